# Optimizing a Trainium2 kernel written in Bass

```python
import math
import jax, jax.numpy as jnp
from jax import lax
import numpy as np

D_MODEL = 1024
BATCH = 4
SEQ = 8192
DEPTH = 2

MEM_LEN = 256
HEAD_DIM = 64
SELF_WIDTH = 3 * D_MODEL // 4
N_MOBA_HEADS = SELF_WIDTH // HEAD_DIM
N_DIFF_HEADS = SELF_WIDTH // (2 * HEAD_DIM)
N_MEM_HEADS = 4
MEM_WIDTH = N_MEM_HEADS * HEAD_DIM
A_IN = 3 * SELF_WIDTH + MEM_WIDTH
B_IN = SELF_WIDTH + MEM_WIDTH
D_FF = 4 * D_MODEL
MOBA_BLOCK = 256
MOBA_TOPK = 3
Q_BLOCK = 128
MOBA_Q_CHUNK = 32
ROPE_THETA = 10000.0
EPS = 1e-6
NEG = -1e30
N_A_LAYERS = DEPTH // 2
N_B_LAYERS = DEPTH - N_A_LAYERS

kernel_name = "yoco_diffattn_moba_hybrid"


def _rmsnorm(x, g):
    xf = x.astype(jnp.float32)
    y = xf * lax.rsqrt(jnp.mean(xf * xf, axis=-1, keepdims=True) + EPS)
    return (y * g.astype(jnp.float32)).astype(x.dtype)


def _rope_tables(n):
    half = HEAD_DIM // 2
    inv = 1.0 / (ROPE_THETA ** (jnp.arange(half, dtype=jnp.float32) / half))
    ang = jnp.arange(n, dtype=jnp.float32)[:, None] * inv[None, :]
    return jnp.cos(ang), jnp.sin(ang)


def _apply_rope(x, cos, sin):
    half = HEAD_DIM // 2
    shp = (x.shape[1],) + (1,) * (x.ndim - 3) + (half,)
    c = cos.reshape(shp)
    s = sin.reshape(shp)
    xf = x.astype(jnp.float32)
    x1, x2 = xf[..., :half], xf[..., half:]
    return jnp.concatenate([x1 * c - x2 * s, x2 * c + x1 * s], axis=-1).astype(x.dtype)


def _sq_relu_mlp(h, w_up, w_down):
    u = jax.nn.relu(h @ w_up)
    return (u * u) @ w_down


def _memory_attention(qm, mem, mem_norm, w_mem_kv):
    B, S = qm.shape[:2]
    M = mem.shape[1]
    kv = _rmsnorm(mem, mem_norm) @ w_mem_kv
    mk = kv[..., :MEM_WIDTH].reshape(B, M, N_MEM_HEADS, HEAD_DIM).astype(jnp.float32)
    mv = kv[..., MEM_WIDTH:].reshape(B, M, N_MEM_HEADS, HEAD_DIM).astype(jnp.float32)
    s = jnp.einsum('bshd,bmhd->bhsm', qm.astype(jnp.float32), mk) * (HEAD_DIM ** -0.5)
    p = jax.nn.softmax(s, axis=-1)
    o = jnp.einsum('bhsm,bmhd->bshd', p, mv)
    return o.reshape(B, S, MEM_WIDTH).astype(qm.dtype)


def _diff_attention(q, k, v, lam, lam_init, subln):
    B, S, H, _, d = q.shape
    nq = S // Q_BLOCK
    kf = k.astype(jnp.float32)
    vf = v.astype(jnp.float32)
    qb = q.astype(jnp.float32).reshape(B, nq, Q_BLOCK, H, 2, d).transpose(1, 0, 2, 3, 4, 5)
    kpos = jnp.arange(S)
    scale = d ** -0.5

    def block(args):
        qi, i = args
        s = jnp.einsum('bqhcd,bkhcd->bhcqk', qi, kf) * scale
        qpos = i * Q_BLOCK + jnp.arange(Q_BLOCK)
        s = jnp.where(kpos[None, :] <= qpos[:, None], s, NEG)
        p = jax.nn.softmax(s, axis=-1)
        w = p[:, :, 0] - lam * p[:, :, 1]
        return jnp.einsum('bhqk,bkhe->bqhe', w, vf)

    o = lax.map(block, (qb, jnp.arange(nq)))
    o = o.transpose(1, 0, 2, 3, 4).reshape(B, S, H, 2 * d)
    o = _rmsnorm(o, subln) * (1.0 - lam_init)
    return o.reshape(B, S, H * 2 * d).astype(v.dtype)


def _moba_attention(q, kbh, vbh, kmean):
    B, S, H, d = q.shape
    NB = kbh.shape[2]
    ksel = min(MOBA_TOPK, NB)
    C = MOBA_Q_CHUNK
    nc = S // C
    qc = q.astype(jnp.float32).transpose(0, 2, 1, 3).reshape(B, H, nc, C, d).transpose(2, 0, 1, 3, 4)
    b_ix = jnp.arange(B)[:, None, None, None]
    h_ix = jnp.arange(H)[None, :, None, None]
    scale = d ** -0.5

    def chunk(args):
        qi, i = args
        qpos = i * C + jnp.arange(C)
        cur = (i * C) // MOBA_BLOCK
        gate = jnp.einsum('bhqd,bhnd->bhqn', qi, kmean)
        gate = jnp.where(jnp.arange(NB) < cur, gate, NEG)
        _, idx = lax.top_k(gate, ksel)
        valid = idx < cur
        k_sel = kbh[b_ix, h_ix, idx]
        v_sel = vbh[b_ix, h_ix, idx]
        s_sel = jnp.einsum('bhqd,bhqnkd->bhqnk', qi, k_sel) * scale
        s_sel = jnp.where(valid[..., None], s_sel, NEG).reshape(B, H, C, ksel * MOBA_BLOCK)
        k_own = lax.dynamic_index_in_dim(kbh, cur, axis=2, keepdims=False)
        v_own = lax.dynamic_index_in_dim(vbh, cur, axis=2, keepdims=False)
        s_own = jnp.einsum('bhqd,bhkd->bhqk', qi, k_own) * scale
        kpos = cur * MOBA_BLOCK + jnp.arange(MOBA_BLOCK)
        s_own = jnp.where(kpos[None, :] <= qpos[:, None], s_own, NEG)
        p = jax.nn.softmax(jnp.concatenate([s_sel, s_own], axis=-1), axis=-1)
        p_sel = p[..., :ksel * MOBA_BLOCK].reshape(B, H, C, ksel, MOBA_BLOCK)
        p_own = p[..., ksel * MOBA_BLOCK:]
        return (jnp.einsum('bhqnk,bhqnkd->bhqd', p_sel, v_sel)
                + jnp.einsum('bhqk,bhkd->bhqd', p_own, v_own))

    o = lax.map(chunk, (qc, jnp.arange(nc)))
    return o.transpose(1, 0, 3, 2, 4).reshape(B, S, H * d).astype(q.dtype)


def setup_inputs(seed: int = 0) -> dict:
    key = jax.random.key(seed)
    ks = iter(jax.random.split(key, 40))

    def nrm(shape, scale):
        return jax.random.normal(next(ks), shape, jnp.float32) * scale

    def gain(shape):
        return 1.0 + nrm(shape, 0.02)

    NA, NBL, D = N_A_LAYERS, N_B_LAYERS, D_MODEL
    return {
        "x": nrm((BATCH, SEQ, D), 1.0),
        "mem": nrm((BATCH, MEM_LEN, D), 1.0),
        "a_norm_attn": gain((NA, D)),
        "a_w_in": nrm((NA, D, A_IN), D ** -0.5),
        "a_lambda": nrm((NA, 4, HEAD_DIM), 0.1),
        "a_subln": gain((NA, 2 * HEAD_DIM)),
        "a_mem_norm": gain((NA, D)),
        "a_w_mem_kv": nrm((NA, D, 2 * MEM_WIDTH), D ** -0.5),
        "a_w_out": nrm((NA, D, D), D ** -0.5),
        "a_norm_mlp": gain((NA, D)),
        "a_w_up": nrm((NA, D, D_FF), D ** -0.5),
        "a_w_down": nrm((NA, D_FF, D), D_FF ** -0.5),
        "kv_norm": gain((D,)),
        "w_kv": nrm((D, 2 * SELF_WIDTH), D ** -0.5),
        "b_norm_attn": gain((NBL, D)),
        "b_w_in": nrm((NBL, D, B_IN), D ** -0.5),
        "b_mem_norm": gain((NBL, D)),
        "b_w_mem_kv": nrm((NBL, D, 2 * MEM_WIDTH), D ** -0.5),
        "b_w_out": nrm((NBL, D, D), D ** -0.5),
        "b_norm_mlp": gain((NBL, D)),
        "b_w_up": nrm((NBL, D, D_FF), D ** -0.5),
        "b_w_down": nrm((NBL, D_FF, D), D_FF ** -0.5),
        "final_norm": gain((D,)),
    }


def reference(x, mem, a_norm_attn, a_w_in, a_lambda, a_subln, a_mem_norm, a_w_mem_kv,
              a_w_out, a_norm_mlp, a_w_up, a_w_down, kv_norm, w_kv, b_norm_attn, b_w_in,
              b_mem_norm, b_w_mem_kv, b_w_out, b_norm_mlp, b_w_up, b_w_down, final_norm):
    B, S, _ = x.shape
    cos, sin = _rope_tables(S)
    kbh = vbh = kmean = None
    for layer in range(DEPTH):
        if layer < N_A_LAYERS:
            i = layer
            h = _rmsnorm(x, a_norm_attn[i])
            z = h @ a_w_in[i]
            w = SELF_WIDTH
            q = z[..., :w].reshape(B, S, N_DIFF_HEADS, 2, HEAD_DIM)
            k = z[..., w:2 * w].reshape(B, S, N_DIFF_HEADS, 2, HEAD_DIM)
            v = z[..., 2 * w:3 * w].reshape(B, S, N_DIFF_HEADS, 2 * HEAD_DIM)
            qm = z[..., 3 * w:].reshape(B, S, N_MEM_HEADS, HEAD_DIM)
            q = _apply_rope(q, cos, sin)
            k = _apply_rope(k, cos, sin)
            lam_init = 0.8 - 0.6 * math.exp(-0.3 * layer)
            lp = a_lambda[i].astype(jnp.float32)
            lam = jnp.exp(jnp.sum(lp[0] * lp[1])) - jnp.exp(jnp.sum(lp[2] * lp[3])) + lam_init
            o_self = _diff_attention(q, k, v, lam, lam_init, a_subln[i])
            o_mem = _memory_attention(qm, mem, a_mem_norm[i], a_w_mem_kv[i])
            x = x + jnp.concatenate([o_self, o_mem], axis=-1) @ a_w_out[i]
            x = x + _sq_relu_mlp(_rmsnorm(x, a_norm_mlp[i]), a_w_up[i], a_w_down[i])
        else:
            j = layer - N_A_LAYERS
            if j == 0:
                hk = _rmsnorm(x, kv_norm)
                kv = hk @ w_kv
                ks_ = _apply_rope(kv[..., :SELF_WIDTH].reshape(B, S, N_MOBA_HEADS, HEAD_DIM), cos, sin)
                vs_ = kv[..., SELF_WIDTH:].reshape(B, S, N_MOBA_HEADS, HEAD_DIM)
                n_blk = -(-S // MOBA_BLOCK)
                pad = n_blk * MOBA_BLOCK - S
                padw = ((0, 0), (0, pad), (0, 0), (0, 0))
                kbh = jnp.pad(ks_.astype(jnp.float32), padw).reshape(
                    B, n_blk, MOBA_BLOCK, N_MOBA_HEADS, HEAD_DIM).transpose(0, 3, 1, 2, 4)
                vbh = jnp.pad(vs_.astype(jnp.float32), padw).reshape(
                    B, n_blk, MOBA_BLOCK, N_MOBA_HEADS, HEAD_DIM).transpose(0, 3, 1, 2, 4)
                kmean = jnp.mean(kbh, axis=3)
            h = _rmsnorm(x, b_norm_attn[j])
            z = h @ b_w_in[j]
            q = _apply_rope(z[..., :SELF_WIDTH].reshape(B, S, N_MOBA_HEADS, HEAD_DIM), cos, sin)
            qm = z[..., SELF_WIDTH:].reshape(B, S, N_MEM_HEADS, HEAD_DIM)
            o_self = _moba_attention(q, kbh, vbh, kmean).astype(x.dtype)
            o_mem = _memory_attention(qm, mem, b_mem_norm[j], b_w_mem_kv[j])
            x = x + jnp.concatenate([o_self, o_mem], axis=-1) @ b_w_out[j]
            x = x + _sq_relu_mlp(_rmsnorm(x, b_norm_mlp[j]), b_w_up[j], b_w_down[j])
    return _rmsnorm(x, final_norm)
```

```python
import math
import numpy as np
import concourse.bass as bass
import concourse.mybir as mybir
from concourse.bass_utils import run_bass_kernel_spmd

F32 = mybir.dt.float32
BF16 = mybir.dt.bfloat16
AF = mybir.ActivationFunctionType
ALU = mybir.AluOpType
AX = mybir.AxisListType

D = 1024
HD = 64
SW = 768
MW = 256
DFF = 4096
MEM = 256
EPS = 1e-6
LAM_INIT = 0.8 - 0.6 * math.exp(-0.3 * 0)
NEGB = 30000.0

ENGS = ["pe", "act", "dve", "pool", "sp"]


class Dep:
    __slots__ = ("w", "r")

    def __init__(self):
        self.w = None
        self.r = {}


class Tile:
    __slots__ = ("t", "d")

    def __init__(self, t):
        self.t = t
        self.d = Dep()


class _Rec:
    def __getattr__(self, name):
        def f(*a, **k):
            self.call = (name, a, k)
            return self
        return f


class Prog:
    def __init__(self, nc, n_dma_sems=40, self_sync=("act", "dve", "pool")):
        self.nc = nc
        self.ins = {e: [] for e in ENGS}
        self.sems = {}
        for e in ENGS:
            self.sems[("e", e)] = nc.alloc_semaphore("s_" + e)
        self.nd = n_dma_sems
        for i in range(n_dma_sems):
            self.sems[("d", i)] = nc.alloc_semaphore("d%d" % i)
        self.cnt = {k: 0 for k in self.sems}
        self.dnext = 0
        self.waited = {}
        self.self_sync = set(self_sync)
        self.nops = 0

    def _wait(self, eng, k, v):
        if self.waited.get((eng, k), 0) >= v:
            return
        self.waited[(eng, k)] = v
        self.ins[eng].append(("wait", k, v))

    def _need(self, eng, deps_r, deps_w):
        need = {}
        for d in deps_r:
            if d.w is not None:
                k, v = d.w
                if need.get(k, 0) < v:
                    need[k] = v
        for d in deps_w:
            if d.w is not None:
                k, v = d.w
                if need.get(k, 0) < v:
                    need[k] = v
            for k, v in d.r.items():
                if need.get(k, 0) < v:
                    need[k] = v
        for k, v in need.items():
            if k == ("e", eng) and eng not in self.self_sync:
                continue
            self._wait(eng, k, v)

    def op(self, eng, fn, reads=(), writes=()):
        self._need(eng, reads, writes)
        k = ("e", eng)
        self.cnt[k] += 1
        v = self.cnt[k]
        rec = _Rec()
        fn(rec)
        self.ins[eng].append(("op", rec.call, k, 1))
        for d in reads:
            d.r[k] = v
        for d in writes:
            d.w = (k, v)
            d.r = {}
        self.nops += 1

    def dma(self, out, in_, reads=(), writes=(), q="sp", **kw):
        i = self.dnext
        self.dnext = (self.dnext + 1) % self.nd
        k = ("d", i)
        prev = self.cnt[k]
        if prev > 0:
            self._wait(q, k, prev)
        self._need(q, reads, writes)
        self.cnt[k] += 16
        v = self.cnt[k]
        kw = dict(kw)
        kw.update(out=out, in_=in_)
        self.ins[q].append(("op", ("dma_start", (), kw), k, 16))
        for d in reads:
            d.r[k] = v
        for d in writes:
            d.w = (k, v)
            d.r = {}
        self.nops += 1

    def collective(self, kind, op, groups, ins, outs):
        k = ("c", 0)
        if k not in self.sems:
            self.sems[k] = self.nc.alloc_semaphore("cc_sem")
            self.cnt[k] = 0
        self.cnt[k] += 16
        self.ins["pool"].append(("op", ("collective_compute", (kind, op), dict(replica_groups=groups, ins=ins, outs=outs)), k, 16))
        self.nops += 1

    def barrier(self, engs=ENGS):
        snap = dict(self.cnt)
        for eng in engs:
            for k, v in snap.items():
                if v <= 0:
                    continue
                if k == ("e", eng) and eng not in self.self_sync:
                    continue
                self._wait(eng, k, v)

    def emit(self):
        nc = self.nc
        sems = self.sems
        ins = self.ins
        with nc.Block() as block:
            def run(e, lst):
                pend = []
                for it in lst:
                    if it[0] == "wait":
                        pend.append(it)
                    else:
                        for w in pend[:-1]:
                            e.wait_ge(sems[w[1]], w[2])
                        r = getattr(e, it[1][0])(*it[1][1], **it[1][2])
                        if pend:
                            r = r._wait_ge(sems[pend[-1][1]], pend[-1][2])
                        r.then_inc(sems[it[2]], it[3])
                        pend = []
                for w in pend:
                    e.wait_ge(sems[w[1]], w[2])

            @block.tensor
            def _(e):
                run(e, ins["pe"])

            @block.scalar
            def _(e):
                run(e, ins["act"])

            @block.vector
            def _(e):
                run(e, ins["dve"])

            @block.gpsimd
            def _(e):
                run(e, ins["pool"])

            @block.sync
            def _(e):
                run(e, ins["sp"])


def bcast_ap(ap, pos, count):
    lst = [list(x) for x in ap.ap]
    lst.insert(pos, [0, count])
    return bass.AP(ap.tensor, ap.offset, lst)


def own_tiles(NT, r):
    res = []
    for g in range(NT // 4):
        res += [4 * g, 4 * g + 3] if r == 0 else [4 * g + 1, 4 * g + 2]
    return res


def tmax_of(j):
    return 4 * (j // 2) + (1 if j % 2 == 0 else 3)


class Ctx:
    pass


def next_stg(C):
    st = C.stg[C.stg_i % len(C.stg)]
    C.stg_i += 1
    return st


def phase_begin(C, nstg=3):
    C.stg = [C.sb("stg%d" % i, [128, 1024], F32) for i in range(nstg)]
    C.stg_i = 0


def phase_end(C):
    C.P.barrier()
    C.nc.sbuf_base = C.sb_mark
    C.nc.psum_base = C.ps_mark


_rr = [0]


def cast_op(C, o, i, rd, wr, gcol=None, engs=("act", "dve", "pool")):
    P = C.P
    eng = engs[_rr[0] % len(engs)]
    _rr[0] += 1
    if gcol is None:
        if eng == "act":
            P.op(eng, lambda e: e.activation(out=o, in_=i, func=AF.Copy), rd, wr)
        else:
            P.op(eng, lambda e: e.tensor_copy(out=o, in_=i), rd, wr)
    else:
        if eng == "act":
            P.op(eng, lambda e: e.activation(out=o, in_=i, func=AF.Copy, scale=gcol), rd, wr)
        else:
            P.op(eng, lambda e: e.tensor_scalar(out=o, in0=i, scalar1=gcol, scalar2=None, op0=ALU.mult), rd, wr)


def load_w(C, dst, src, KC, N, gain=None):
    P = C.P
    for kc in range(KC):
        for n0 in range(0, N, 1024):
            n1 = min(N, n0 + 1024)
            w = n1 - n0
            st = next_stg(C)
            P.dma(st.t[:, 0:w], src[kc * 128:(kc + 1) * 128, n0:n1], writes=[st.d])
            rd = [st.d] + ([gain.d] if gain is not None else [])
            cast_op(C, dst.t[:, kc, n0:n1], st.t[:, 0:w], rd, [dst.d], None if gain is None else gain.t[:, kc:kc + 1])


def load_w_chunk(C, Wf, ci, src, gain, Wsw):
    P = C.P
    st = next_stg(C)
    sv = st.t[:, 0:1024].rearrange("p (c n) -> p c n", n=128)
    P.dma(sv, src.rearrange("(c p) n -> p c n", p=128), writes=[st.d])
    for kc in range(8):
        gcol = gain.t[:, kc:kc + 1]
        cast_op(C, Wf.t[:, kc, ci * 128:(ci + 1) * 128], sv[:, kc, :], [st.d, gain.d], [Wf.d], gcol)
        if Wsw is not None:
            for half in range(2):
                o2 = Wsw.t[:, kc, ci * 128:(ci + 1) * 128].rearrange("p (b t i) -> p b t i", t=2, i=32)[:, :, half, :]
                i2 = sv[:, kc, :].rearrange("p (b t i) -> p b t i", t=2, i=32)[:, :, 1 - half, :]
                cast_op(C, o2, i2, [st.d, gain.d], [Wsw.d], gcol, engs=("dve", "pool"))


def norm_T(C, xres, nsub, hT, pT):
    P = C.P
    ss, rs = C.ss, C.rs
    for s in range(nsub):
        P.op("act", lambda e, s=s: e.activation(out=C.junk.t[:], in_=xres.t[:, s, :], func=AF.Square, accum_out=ss.t[:, s:s + 1]),
             [xres.d], [C.junk.d, ss.d])
    P.op("act", lambda e: e.activation(out=ss.t[:, 0:nsub], in_=ss.t[:, 0:nsub], func=AF.Sqrt, scale=1.0 / D, bias=EPS), [ss.d], [ss.d])
    P.op("dve", lambda e: e.reciprocal(out=rs.t[:, 0:nsub], in_=ss.t[:, 0:nsub]), [ss.d], [rs.d])
    for s in range(nsub):
        xb = C.xb[s % 2]
        pt = pT[s % len(pT)]
        P.op("dve", lambda e, s=s, xb=xb: e.tensor_scalar(out=xb.t[:], in0=xres.t[:, s, :], scalar1=rs.t[:, s:s + 1], scalar2=None, op0=ALU.mult),
             [xres.d, rs.d], [xb.d])
        for kc in range(8):
            P.op("pe", lambda e, kc=kc, xb=xb, pt=pt: e.transpose(out=pt.t[:, kc * 128:(kc + 1) * 128], in_=xb.t[:, kc * 128:(kc + 1) * 128], identity=C.idb.t[:]),
                 [xb.d, C.idb.d], [pt.d])
        P.op("act", lambda e, s=s, pt=pt: e.activation(out=hT.t[:, :, s * 128:(s + 1) * 128], in_=pt.t[:].rearrange("p (c t) -> p c t", c=8), func=AF.Copy),
             [pt.d], [hT.d])


def proj_phase(C, x_src, ntiles, gain, fm, rope_d, tm, gate=None):
    P, sb, ps = C.P, C.sb, C.ps
    phase_begin(C)
    nfm = len(fm)
    nrope = sum(1 for f in fm if f["rope"] is not None)
    rope_ds = rope_d if isinstance(rope_d, (list, tuple)) else [rope_d]
    nrt = len(rope_ds)
    Wf = sb("Wf", [128, 8, 128 * nfm], BF16)
    Wsw = sb("Wsw", [128, 8, 128 * nrope], BF16) if nrope else None
    for ci, f in enumerate(fm):
        assert (f["rope"] is not None) == (ci < nrope)
        load_w_chunk(C, Wf, ci, f["w"], gain, Wsw if f["rope"] is not None else None)
    Wv = None
    if tm is not None:
        Wv = sb("Wv", [128, 8, SW], BF16)
        load_w(C, Wv, tm["w"], 8, SW, gain)
    xres = [sb("xres%d" % i, [128, 4, D], F32) for i in range(2)]
    hT = [sb("hT%d" % i, [128, 8, 512], BF16) for i in range(2)]
    rope = [sb("rope%d" % i, [128, nrt, 2, 512], F32) for i in range(2)]
    t1 = [sb("t1_%d" % i, [128, 512], F32) for i in range(2)]
    t2 = [sb("t2_%d" % i, [128, 512], F32) for i in range(2)]
    kf = [sb("kf%d" % i, [128, 512], F32) for i in range(2)]
    kb = [sb("kb%d" % i, [128, 512], BF16) for i in range(3)]
    nab = 1 if gate is not None else 2
    pT = [ps("pT%d" % i, [128, 1024], BF16) for i in range(2)]
    pA = [ps("pA%d" % i, [128, 512], F32) for i in range(nab)]
    pB = [ps("pB%d" % i, [128, 512], F32) for i in range(nab)]
    pV = [ps("pV%d" % i, [128, 512], F32) for i in range(2)] if tm is not None else None
    vsb = [sb("vsb%d" % i, [128, 4, SW], BF16) for i in range(2)] if tm is not None else None
    kms = None
    if any("kmean" in f for f in fm):
        kms = sb("kms", [128, 6, 32], F32)
        P.op("pool", lambda e: e.memset(kms.t[:], 0.0), [], [kms.d])
    if gate is not None:
        kmt = sb("kmt", [128, 6, 32], F32)
        P.dma(kmt.t[:], gate["km"].ap().rearrange("c p n -> p c n"), writes=[kmt.d])
        kmz = sb("kmz", [128, 12, 32], F32)
        P.op("pool", lambda e: e.memset(kmz.t[:], 0.0), [], [kmz.d])
        for hd in range(12):
            c6, po = hd // 2, (hd % 2) * 64
            P.op("dve", lambda e, hd=hd, c6=c6, po=po: e.tensor_copy(out=kmz.t[po:po + 64, hd, :], in_=kmt.t[po:po + 64, c6, :]), [kmt.d, kmz.d], [kmz.d])
        gbt = [sb("gbt%d" % i, [128, 4, 32], F32) for i in range(2)]
        t3t = [sb("t3t%d" % i, [128, 4, 32], F32) for i in range(2)]
        q32 = sb("q32", [128, 6, 512], F32)
        G = sb("G", [128, 12, 32], F32)
        top8 = sb("top8", [128, 12, 8], F32)
        thr = sb("thr", [128, 12], F32)
        fin = sb("fin", [128, 12, 32], F32)
        biasT = [sb("biasT%d" % i, [128, 512], BF16) for i in range(2)]
        pG = ps("pG", [128, 512], F32)
        pGT = [ps("pGT%d" % i, [128, 512], F32) for i in range(3)]
    ctr = dict(ab=0, kb=0)

    def loads(T):
        xr = xres[T % 2]
        P.dma(xr.t[:], x_src[T * 512:(T + 1) * 512, :].rearrange("(s p) d -> p s d", p=128), writes=[xr.d])
        if nrope:
            rp = rope[T % 2]
            for ri, rd_ in enumerate(rope_ds):
                P.dma(rp.t[:, ri], rd_.ap()[:, :, T * 512:(T + 1) * 512].rearrange("c p t -> p c t"), writes=[rp.d])
        if gate is not None:
            P.dma(gbt[T % 2].t[:], gate["gb"].ap()[T * 512:(T + 1) * 512, :].rearrange("(s p) n -> p s n", p=128), writes=[gbt[T % 2].d])
            P.dma(t3t[T % 2].t[:], gate["t3"].ap()[T * 512:(T + 1) * 512, :].rearrange("(s p) n -> p s n", p=128), writes=[t3t[T % 2].d])

    loads(0)
    for T in range(ntiles):
        if T + 1 < ntiles:
            loads(T + 1)
        xr, h, rp = xres[T % 2], hT[T % 2], rope[T % 2]
        tsl = slice(T * 512, (T + 1) * 512)
        norm_T(C, xr, 4, h, pT)
        for ci, f in enumerate(fm):
            a = pA[ctr["ab"] % nab]
            for kc in range(8):
                P.op("pe", lambda e, kc=kc, ci=ci, a=a, h=h: e.matmul(a.t[:], lhsT=Wf.t[:, kc, ci * 128:(ci + 1) * 128], rhs=h.t[:, kc, :], start=(kc == 0), stop=(kc == 7)),
                     [Wf.d, h.d], [a.d])
            k16 = kb[ctr["kb"] % 3]
            ctr["kb"] += 1
            if f["rope"] is not None:
                ri = f["rope"]
                b = pB[ctr["ab"] % nab]
                for kc in range(8):
                    P.op("pe", lambda e, kc=kc, ci=ci, b=b, h=h: e.matmul(b.t[:], lhsT=Wsw.t[:, kc, ci * 128:(ci + 1) * 128], rhs=h.t[:, kc, :], start=(kc == 0), stop=(kc == 7)),
                         [Wsw.d, h.d], [b.d])
                u1, u2 = t1[ctr["ab"] % 2], t2[ctr["ab"] % 2]
                P.op("dve", lambda e, a=a, u1=u1, rp=rp, ri=ri: e.tensor_tensor(out=u1.t[:], in0=a.t[:], in1=rp.t[:, ri, 0, :], op=ALU.mult), [a.d, rp.d], [u1.d])
                P.op("dve", lambda e, b=b, u2=u2, rp=rp, ri=ri: e.tensor_tensor(out=u2.t[:], in0=b.t[:], in1=rp.t[:, ri, 1, :], op=ALU.mult), [b.d, rp.d], [u2.d])
                if "gate" in f:
                    c6 = f["gate"]
                    P.op("pool", lambda e, u1=u1, u2=u2, c6=c6: e.tensor_tensor(out=q32.t[:, c6, :], in0=u1.t[:], in1=u2.t[:], op=ALU.add), [u1.d, u2.d], [q32.d])
                    P.op("act", lambda e, k16=k16, c6=c6: e.activation(out=k16.t[:], in_=q32.t[:, c6, :], func=AF.Copy), [q32.d], [k16.d])
                    for hh in range(2):
                        P.dma(gate["QB"].ap()[2 * c6 + hh, 0:64, tsl], k16.t[hh * 64:(hh + 1) * 64, :], reads=[k16.d], writes=[Dep()])
                else:
                    kk = kf[ctr["ab"] % 2]
                    P.op("pool", lambda e, u1=u1, u2=u2, kk=kk: e.tensor_tensor(out=kk.t[:], in0=u1.t[:], in1=u2.t[:], op=ALU.add), [u1.d, u2.d], [kk.d])
                    P.op("act", lambda e, k16=k16, kk=kk: e.activation(out=k16.t[:], in_=kk.t[:], func=AF.Copy), [kk.d], [k16.d])
                    P.dma(f["out"][:, tsl], k16.t[:], reads=[k16.d], writes=[Dep()])
                    if "kmean" in f:
                        P.op("dve", lambda e, kk=kk, ci=ci, T=T: e.tensor_reduce(out=kms.t[:, ci, 2 * T:2 * T + 2], in_=kk.t[:].rearrange("p (b k) -> p b k", k=256), axis=AX.X, op=ALU.add),
                             [kk.d], [kms.d])
            else:
                sc = f.get("scale", 1.0)
                P.op("act", lambda e, k16=k16, a=a, sc=sc: e.activation(out=k16.t[:], in_=a.t[:], func=AF.Copy, scale=sc), [a.d], [k16.d])
                P.dma(f["out"][:, tsl], k16.t[:], reads=[k16.d], writes=[Dep()])
            ctr["ab"] += 1
        if tm is not None:
            vs = vsb[T % 2]
            for s in range(4):
                pv0, pv1 = pV[0], pV[1]
                for kc in range(8):
                    P.op("pe", lambda e, kc=kc, s=s, h=h, pv0=pv0: e.matmul(pv0.t[:], lhsT=h.t[:, kc, s * 128:(s + 1) * 128], rhs=Wv.t[:, kc, 0:512], start=(kc == 0), stop=(kc == 7)),
                         [Wv.d, h.d], [pv0.d])
                for kc in range(8):
                    P.op("pe", lambda e, kc=kc, s=s, h=h, pv1=pv1: e.matmul(pv1.t[:, 0:256], lhsT=h.t[:, kc, s * 128:(s + 1) * 128], rhs=Wv.t[:, kc, 512:768], start=(kc == 0), stop=(kc == 7)),
                         [Wv.d, h.d], [pv1.d])
                P.op("act", lambda e, s=s, vs=vs, pv0=pv0: e.activation(out=vs.t[:, s, 0:512], in_=pv0.t[:], func=AF.Copy), [pv0.d], [vs.d])
                P.op("dve", lambda e, s=s, vs=vs, pv1=pv1: e.tensor_copy(out=vs.t[:, s, 512:768], in_=pv1.t[:, 0:256]), [pv1.d], [vs.d])
            P.dma(tm["out"][tsl, :].rearrange("(s p) e -> p s e", p=128), vs.t[:], reads=[vs.d], writes=[Dep()])
        if gate is not None:
            gbs, t3s = gbt[T % 2], t3t[T % 2]
            for s in range(4):
                for hd in range(12):
                    c6, po = hd // 2, (hd % 2) * 64
                    P.op("pe", lambda e, s=s, hd=hd, c6=c6, po=po: e.matmul(pG.t[:, hd * 32:(hd + 1) * 32], lhsT=q32.t[:, c6, s * 128:(s + 1) * 128],
                                                                          rhs=kmz.t[:, hd, :], start=True, stop=True),
                         [q32.d, kmz.d], [pG.d])
                P.op("dve", lambda e, s=s, gbs=gbs: e.tensor_tensor(out=G.t[:], in0=pG.t[:, 0:384].rearrange("p (h n) -> p h n", n=32),
                                                                  in1=bcast_ap(gbs.t[:, s, :], 1, 12), op=ALU.add), [pG.d, gbs.d], [G.d])
                for hd in range(12):
                    P.op("dve", lambda e, hd=hd: e.max(out=top8.t[:, hd, :], in_=G.t[:, hd, :]), [G.d], [top8.d])
                P.op("dve", lambda e: e.tensor_scalar(out=thr.t[:], in0=top8.t[:, :, 2], scalar1=-1e29, scalar2=None, op0=ALU.max), [top8.d], [thr.d])
                P.op("dve", lambda e: e.tensor_tensor(out=fin.t[:], in0=G.t[:], in1=bcast_ap(thr.t[:], 2, 32), op=ALU.is_ge), [G.d, thr.d], [fin.d])
                P.op("dve", lambda e, s=s, t3s=t3s: e.tensor_tensor(out=fin.t[:], in0=fin.t[:], in1=bcast_ap(t3s.t[:, s, :], 1, 12), op=ALU.add), [fin.d, t3s.d], [fin.d])
                for grp in range(3):
                    P.op("pe", lambda e, s=s, grp=grp: e.matmul(pGT[grp].t[:, s * 128:(s + 1) * 128],
                                                               lhsT=fin.t[:, 4 * grp:4 * grp + 4, :].rearrange("p h n -> p (h n)"), rhs=C.idf.t[:], start=True, stop=True),
                         [fin.d, C.idf.d], [pGT[grp].d])
            for grp in range(3):
                bt = biasT[grp % 2]
                P.op("act", lambda e, grp=grp, bt=bt: e.activation(out=bt.t[:], in_=pGT[grp].t[:], func=AF.Copy, scale=NEGB), [pGT[grp].d], [bt.d])
                for hl in range(4):
                    P.dma(gate["QB"].ap()[4 * grp + hl, 64:96, tsl], bt.t[hl * 32:(hl + 1) * 32, :], reads=[bt.d], writes=[Dep()])
    if kms is not None:
        P.op("dve", lambda e: e.tensor_scalar(out=kms.t[:], in0=kms.t[:], scalar1=1.0 / 256.0, scalar2=None, op0=ALU.mult), [kms.d], [kms.d])
        for ci, f in enumerate(fm):
            if "kmean" in f:
                P.dma(f["kmean"], kms.t[:, ci, :], reads=[kms.d], writes=[Dep()])
    phase_end(C)


def mem_phase(C, mem_d, w, gain, QM, OM, ntiles):
    P, sb, ps = C.P, C.sb, C.ps
    phase_begin(C)
    NO = ntiles
    Wm = sb("Wm", [128, 8, 2 * MW], BF16)
    load_w(C, Wm, w, 8, 2 * MW, gain)
    xm = sb("xm", [128, 2, D], F32)
    P.dma(xm.t[:], mem_d.ap().rearrange("(s p) d -> p s d", p=128), writes=[xm.d])
    mT = sb("mT", [128, 8, 256], BF16)
    pT = [ps("pT0", [128, 1024], BF16)]
    norm_T(C, xm, 2, mT, pT)
    MKT = sb("MKT", [128, 2, 256], BF16)
    MV = sb("MV", [128, 2, 256], BF16)
    pa = ps("pa", [128, 512], F32)
    for c in range(2):
        for kc in range(8):
            P.op("pe", lambda e, kc=kc, c=c: e.matmul(pa.t[:, 0:256], lhsT=Wm.t[:, kc, c * 128:(c + 1) * 128], rhs=mT.t[:, kc, :], start=(kc == 0), stop=(kc == 7)),
                 [Wm.d, mT.d], [pa.d])
        P.op("act", lambda e, c=c: e.activation(out=MKT.t[:, c, :], in_=pa.t[:, 0:256], func=AF.Copy), [pa.d], [MKT.d])
    for mt in range(2):
        for kc in range(8):
            P.op("pe", lambda e, kc=kc, mt=mt: e.matmul(pa.t[:, 0:256], lhsT=mT.t[:, kc, mt * 128:(mt + 1) * 128], rhs=Wm.t[:, kc, 256:512], start=(kc == 0), stop=(kc == 7)),
                 [Wm.d, mT.d], [pa.d])
        P.op("act", lambda e, mt=mt: e.activation(out=MV.t[:, mt, :], in_=pa.t[:, 0:256], func=AF.Copy), [pa.d], [MV.d])
    qm = [sb("qm%d" % i, [128, 2, 512], BF16) for i in range(2)]
    Pt = [sb("Pt%d" % i, [128, 512], BF16) for i in range(3)]
    Rs = [sb("R%d" % i, [64, 512], F32) for i in range(2)]
    om = [sb("om%d" % i, [64, 512], BF16) for i in range(2)]
    pS = [ps("pS%d" % i, [128, 512], F32) for i in range(2)]
    pOs = [ps("pO%d" % i, [128, 512], F32) for i in range(2)]
    pLs = [ps("pL%d" % i, [128, 512], F32) for i in range(2)]
    u = 0
    for j in range(NO):
        q = qm[j % 2]
        P.dma(q.t[:], QM.ap()[:, :, j * 512:(j + 1) * 512].rearrange("c p t -> p c t"), writes=[q.d])
        for mh in range(4):
            c, po = mh // 2, (mh % 2) * 64
            pO, pL, R = pOs[(j * 4 + mh) % 2], pLs[(j * 4 + mh) % 2], Rs[(j * 4 + mh) % 2]
            for mt in range(2):
                s_, pt = pS[u % 2], Pt[u % 3]
                u += 1
                P.op("pe", lambda e, s_=s_, q=q, c=c, po=po, mt=mt: e.matmul(s_.t[:], lhsT=MKT.t[po:po + 64, c, mt * 128:(mt + 1) * 128], rhs=q.t[po:po + 64, c, :], start=True, stop=True),
                     [MKT.d, q.d], [s_.d])
                P.op("act", lambda e, s_=s_, pt=pt: e.activation(out=pt.t[:], in_=s_.t[:], func=AF.Exp), [s_.d], [pt.d])
                P.op("pe", lambda e, pt=pt, mt=mt, mh=mh: e.matmul(pO.t[0:64, :], lhsT=MV.t[:, mt, mh * 64:(mh + 1) * 64], rhs=pt.t[:], start=(mt == 0), stop=(mt == 1)),
                     [MV.d, pt.d], [pO.d])
                P.op("pe", lambda e, pt=pt, mt=mt: e.matmul(pL.t[0:64, :], lhsT=C.onesb.t[:, 0:64], rhs=pt.t[:], start=(mt == 0), stop=(mt == 1)),
                     [C.onesb.d, pt.d], [pL.d])
            o_ = om[(j * 4 + mh) % 2]
            P.op("dve", lambda e: e.reciprocal(out=R.t[:], in_=pL.t[0:64, :]), [pL.d], [R.d])
            P.op("dve", lambda e, o_=o_: e.tensor_tensor(out=o_.t[:], in0=pO.t[0:64, :], in1=R.t[:], op=ALU.mult), [pO.d, R.d], [o_.d])
            P.dma(OM.ap()[mh, :, j * 512:(j + 1) * 512], o_.t[:], reads=[o_.d], writes=[Dep()])
    phase_end(C)


def load_cmask(C, dmask_d, cmx_d):
    P = C.P
    dm = C.sb("dmask_sb", [128, 4, 512], BF16)
    for i in range(4):
        st = next_stg(C)
        P.dma(st.t[:, 0:512], dmask_d.ap()[i], writes=[st.d])
        P.op("dve", lambda e, st=st, i=i: e.tensor_copy(out=dm.t[:, i, :], in_=st.t[:, 0:512]), [st.d], [dm.d])
    cx = C.sb("cmx_sb", [128, 4], F32)
    P.dma(cx.t[:], cmx_d.ap(), writes=[cx.d])
    dmb = C.sb("dmaskb_sb", [128, 4, 512], BF16)
    P.op("dve", lambda e: e.tensor_scalar(out=dmb.t[:], in0=dm.t[:], scalar1=-1.0, scalar2=NEGB, op0=ALU.add, op1=ALU.mult), [dm.d], [dmb.d])
    bx = C.sb("bx_sb", [128, 4], F32)
    P.op("dve", lambda e: e.tensor_scalar(out=bx.t[:], in0=cx.t[:], scalar1=-1.0, scalar2=NEGB, op0=ALU.add, op1=ALU.mult), [cx.d], [bx.d])
    return dmb, bx


def key_units(C, gq, j):
    half = C.NT // 2
    res = []
    for rel in range(2):
        g = gq if rel == 0 else 1 - gq
        for jj in range(j + 1):
            for i in range(4):
                m = None
                if jj == j:
                    m = ("d", i) if rel == 0 else ("x", gq * 2 + j % 2)
                res.append(((g * half + jj) * 4 + i, m))
    return res


def apply_mask(C, pt, m, dm, cx, view=None, nrep=1):
    P = C.P
    if m is None:
        return
    ap = pt.t[:] if view is None else view
    if m[0] == "d":
        if nrep == 1:
            P.op("dve", lambda e: e.tensor_tensor(out=ap, in0=ap, in1=dm.t[:, m[1], :], op=ALU.mult), [pt.d, dm.d], [pt.d])
        else:
            ap3 = ap.rearrange("p (r q) -> p r q", r=nrep)
            P.op("dve", lambda e: e.tensor_tensor(out=ap3, in0=ap3, in1=bcast_ap(dm.t[:, m[1], :], 1, nrep), op=ALU.mult), [pt.d, dm.d], [pt.d])
    else:
        P.op("dve", lambda e: e.tensor_scalar(out=ap, in0=ap, scalar1=cx.t[:, m[1]:m[1] + 1], scalar2=None, op0=ALU.mult), [pt.d, cx.d], [pt.d])


def diff_attn_phase(C, KA, VA, QA, OA, a_lambda, a_subln, dmask_d, cmx_d, pre=None):
    P, sb, ps = C.P, C.sb, C.ps
    phase_begin(C)
    if pre is not None:
        pre()
    S, NO, NKT = C.S, C.NT, C.NKT
    half = C.NT // 2
    dm, cx = load_cmask(C, dmask_d, cmx_d)
    lam_t = sb("lam_t", [128, 4, HD], F32)
    P.dma(lam_t.t[:], bass.AP(a_lambda, 0, [[0, 128], [HD, 4], [1, HD]]), writes=[lam_t.d])
    lp = sb("lp", [128, 2, HD], F32)
    lsum = sb("lsum", [128, 2], F32)
    neglam = sb("neglam", [128, 1], F32)
    gsub = sb("gsub", [128, 1], F32)
    P.op("dve", lambda e: e.tensor_tensor(out=lp.t[:, 0, :], in0=lam_t.t[:, 0, :], in1=lam_t.t[:, 1, :], op=ALU.mult), [lam_t.d], [lp.d])
    P.op("dve", lambda e: e.tensor_tensor(out=lp.t[:, 1, :], in0=lam_t.t[:, 2, :], in1=lam_t.t[:, 3, :], op=ALU.mult), [lam_t.d], [lp.d])
    P.op("dve", lambda e: e.tensor_reduce(out=lsum.t[:], in_=lp.t[:], axis=AX.X, op=ALU.add), [lp.d], [lsum.d])
    P.op("act", lambda e: e.activation(out=lsum.t[:], in_=lsum.t[:], func=AF.Exp), [lsum.d], [lsum.d])
    P.op("dve", lambda e: e.tensor_tensor(out=neglam.t[:], in0=lsum.t[:, 1:2], in1=lsum.t[:, 0:1], op=ALU.subtract), [lsum.d], [neglam.d])
    P.op("dve", lambda e: e.tensor_scalar(out=neglam.t[:], in0=neglam.t[:], scalar1=-LAM_INIT, scalar2=None, op0=ALU.add), [neglam.d], [neglam.d])
    P.dma(gsub.t[:], a_subln.ap()[0].rearrange("(p o) -> p o", o=1), writes=[gsub.d])
    P.op("dve", lambda e: e.tensor_scalar(out=gsub.t[:], in0=gsub.t[:], scalar1=1.0 - LAM_INIT, scalar2=None, op0=ALU.mult), [gsub.d], [gsub.d])

    Kh = [sb("Kh%d" % i, [128, S], BF16) for i in range(2)]
    Vh = [sb("Vh%d" % i, [128, NKT, 128], BF16) for i in range(2)]
    Qt = [sb("Qt%d" % i, [128, 512], BF16) for i in range(2)]
    Pt = [sb("Pt%d" % i, [128, 1024], BF16) for i in range(6)]
    ep = {n: sb("ep_" + n, [128, 512], F32) for n in ["R1", "R2", "O1", "O2", "o1", "o2", "o", "sq", "ln", "rstd"]}
    ot = [sb("ot%d" % i, [128, 512], BF16) for i in range(2)]
    acc = [sb("acc%d" % g, [128, 1024], F32) for g in range(2)]
    accS = sb("accS", [128, 1024], F32)
    pend = [None, None]
    pS = [ps("pS%d" % i, [128, 1024], F32) for i in range(2)]
    pO = [ps("pO%d" % i, [128, 512], F32) for i in range(2)]
    pL = [ps("pL%d" % i, [128, 512], F32) for i in range(2)]
    pM = pL[0]

    def load_kv(h):
        k, v = Kh[h % 2], Vh[h % 2]
        for t0 in range(0, S, 2048):
            t1_ = min(S, t0 + 2048)
            P.dma(k.t[:, t0:t1_], KA.ap()[h, :, t0:t1_], writes=[k.d])
        for kt0 in range(0, NKT, 16):
            P.dma(v.t[:, kt0:kt0 + 16, :], VA.ap()[kt0 * 128:(kt0 + 16) * 128, h * 128:(h + 1) * 128].rearrange("(t p) e -> p t e", p=128), writes=[v.d])

    def load_q(h, j):
        q = Qt[(h * NO + j) % 2]
        P.dma(q.t[:], QA.ap()[h, :, j * 512:(j + 1) * 512], writes=[q.d])

    load_kv(0)
    load_q(0, 0)
    gu = 0
    for h in range(6):
        if h + 1 < 6:
            load_kv(h + 1)
        k, v = Kh[h % 2], Vh[h % 2]
        for j in range(NO):
            nxt = h * NO + j + 1
            if nxt < 6 * NO:
                load_q(nxt // NO, nxt % NO)
            q = Qt[(h * NO + j) % 2]
            ku = key_units(C, j // half, j % half)
            n = len(ku)

            def qk(p):
                kt, m = ku[p]
                s2 = pS[(gu + p) % 2]
                dg = (m is not None and m[0] == "d")
                for c in range(2):
                    P.op("pe", lambda e: e.matmul(s2.t[:, c * 512:(c + 1) * 512], lhsT=k.t[c * 64:(c + 1) * 64, kt * 128:(kt + 1) * 128], rhs=q.t[c * 64:(c + 1) * 64, :], start=True, stop=not dg),
                         [k.d, q.d], [s2.d])
                    if dg:
                        P.op("pe", lambda e: e.matmul(s2.t[:, c * 512:(c + 1) * 512], lhsT=C.idb.t[:], rhs=dm.t[:, m[1], :], start=False, stop=True),
                             [C.idb.d, dm.d], [s2.d])

            qk(0)
            qk(1)
            for p in range(n):
                kt, m = ku[p]
                s2, pt = pS[(gu + p) % 2], Pt[(gu + p) % 6]
                if m is not None and m[0] == "x":
                    P.op("act", lambda e: e.activation(out=pt.t[:], in_=s2.t[:], func=AF.Exp, bias=cx.t[:, m[1]:m[1] + 1]), [s2.d, cx.d], [pt.d])
                else:
                    P.op("act", lambda e: e.activation(out=pt.t[:], in_=s2.t[:], func=AF.Exp), [s2.d], [pt.d])
                for c in range(2):
                    P.op("pe", lambda e: e.matmul(pO[c].t[:], lhsT=v.t[:, kt, :], rhs=pt.t[:, c * 512:(c + 1) * 512], start=(p == 0), stop=(p == n - 1)), [v.d, pt.d], [pO[c].d])
                g = 0 if p % 3 == 0 else 1
                ac = acc[g]
                aeng = ("pool", "dve")[g]
                if p < 2:
                    P.op(aeng, lambda e: e.tensor_copy(out=ac.t[:], in_=pt.t[:]), [pt.d], [ac.d])
                else:
                    P.op(aeng, lambda e: e.tensor_tensor(out=ac.t[:], in0=ac.t[:], in1=pt.t[:], op=ALU.add), [pt.d, ac.d], [ac.d])
                if p + 2 < n:
                    qk(p + 2)
                if p == 1 and pend[0] is not None:
                    pend[0]()
                    pend[0] = None
                if p == 5 and pend[1] is not None:
                    pend[1]()
                    pend[1] = None
            gu += n
            P.op("dve", lambda e: e.tensor_tensor(out=accS.t[:], in0=acc[0].t[:], in1=acc[1].t[:], op=ALU.add), [acc[0].d, acc[1].d], [accS.d])
            P.op("act", lambda e: e.activation(out=ep["O1"].t[:], in_=pO[0].t[:], func=AF.Copy), [pO[0].d], [ep["O1"].d])
            P.op("act", lambda e: e.activation(out=ep["O2"].t[:], in_=pO[1].t[:], func=AF.Copy), [pO[1].d], [ep["O2"].d])

            def part2a():
                for c in range(2):
                    P.op("pe", lambda e: e.matmul(pL[c].t[:], lhsT=C.ones1.t[:], rhs=accS.t[:, c * 512:(c + 1) * 512], start=True, stop=True), [C.ones1.d, accS.d], [pL[c].d])
                for c, rn in ((0, "R1"), (1, "R2")):
                    P.op("act", lambda e: e.activation(out=ep[rn].t[:], in_=pL[c].t[:], func=AF.Ln), [pL[c].d], [ep[rn].d])
                    P.op("act", lambda e: e.activation(out=ep[rn].t[:], in_=ep[rn].t[:], func=AF.Exp, scale=-1.0), [ep[rn].d], [ep[rn].d])
                P.op("pool", lambda e: e.tensor_tensor(out=ep["o1"].t[:], in0=ep["O1"].t[:], in1=ep["R1"].t[:], op=ALU.mult), [ep["O1"].d, ep["R1"].d], [ep["o1"].d])
                P.op("dve", lambda e: e.tensor_tensor(out=ep["o2"].t[:], in0=ep["O2"].t[:], in1=ep["R2"].t[:], op=ALU.mult), [ep["O2"].d, ep["R2"].d], [ep["o2"].d])
                P.op("dve", lambda e: e.scalar_tensor_tensor(out=ep["o"].t[:], in0=ep["o2"].t[:], scalar=neglam.t[:, 0:1], in1=ep["o1"].t[:], op0=ALU.mult, op1=ALU.add),
                     [ep["o2"].d, ep["o1"].d, neglam.d], [ep["o"].d])
                P.op("pool", lambda e: e.tensor_tensor(out=ep["sq"].t[:], in0=ep["o"].t[:], in1=ep["o"].t[:], op=ALU.mult), [ep["o"].d], [ep["sq"].d])

            def part2b(h=h, j=j):
                P.op("pe", lambda e: e.matmul(pM.t[:], lhsT=C.onesf.t[:], rhs=ep["sq"].t[:], start=True, stop=True), [C.onesf.d, ep["sq"].d], [pM.d])
                P.op("act", lambda e: e.activation(out=ep["ln"].t[:], in_=pM.t[:], func=AF.Ln, bias=EPS), [pM.d], [ep["ln"].d])
                P.op("act", lambda e: e.activation(out=ep["rstd"].t[:], in_=ep["ln"].t[:], func=AF.Exp, scale=-0.5), [ep["ln"].d], [ep["rstd"].d])
                o16 = ot[(h * NO + j) % 2]
                P.op("dve", lambda e: e.scalar_tensor_tensor(out=o16.t[:], in0=ep["o"].t[:], scalar=gsub.t[:, 0:1], in1=ep["rstd"].t[:], op0=ALU.mult, op1=ALU.mult),
                     [ep["o"].d, ep["rstd"].d, gsub.d], [o16.d])
                P.dma(OA.ap()[h, :, j * 512:(j + 1) * 512], o16.t[:], reads=[o16.d], writes=[Dep()])

            pend[0], pend[1] = part2a, part2b
    for fn in pend:
        if fn is not None:
            fn()
    phase_end(C)


def moba_attn_phase(C, KBf, VBf, QB, OB, onehot_d, dmask_d, cmx_d, pre=None):
    P, sb, ps = C.P, C.sb, C.ps
    phase_begin(C)
    if pre is not None:
        pre()
    S, NO, NKT = C.S, C.NO, C.NKT
    dm, cx = load_cmask(C, dmask_d, cmx_d)
    Kh = [sb("Kh%d" % i, [128, S], BF16) for i in range(2)]
    Vh = [sb("Vh%d" % i, [128, NKT, 128], BF16) for i in range(2)]
    Osb = [sb("Osb%d" % i, [128, 512], F32) for i in range(2)]
    for b in range(2):
        P.op("pool", lambda e: e.memset(Vh[b].t[:, :, 64:128], 1.0), [], [Vh[b].d])
    Qt = [sb("Qt%d" % i, [128, 512], BF16) for i in range(2)]
    Pt = [sb("Pt%d" % i, [128, 1024], BF16) for i in range(4)]
    R = sb("R", [64, 512], F32)
    ot = [sb("ot%d" % i, [64, 512], BF16) for i in range(2)]
    pS = [ps("pS%d" % i, [128, 1024], F32) for i in range(3)]
    pO = ps("pO", [128, 512], F32)
    pL = ps("pL", [128, 512], F32)
    for b in range(2):
        for n0 in range(0, S, 1024):
            n1 = min(S, n0 + 1024)
            st = next_stg(C)
            P.dma(st.t[64:96, 0:n1 - n0], onehot_d.ap()[:, n0:n1], writes=[st.d])
            P.op("dve", lambda e, st=st, b=b, n0=n0, n1=n1: e.tensor_copy(out=Kh[b].t[64:96, n0:n1], in_=st.t[64:96, 0:n1 - n0]), [st.d], [Kh[b].d])

    def load_kv(h):
        k, v = Kh[h % 2], Vh[h % 2]
        for t0 in range(0, S, 2048):
            t1_ = min(S, t0 + 2048)
            P.dma(k.t[0:64, t0:t1_], KBf.ap()[h // 2, (h % 2) * 64:(h % 2) * 64 + 64, t0:t1_], writes=[k.d])
        for kt0 in range(0, NKT, 16):
            P.dma(v.t[:, kt0:kt0 + 16, 0:64], VBf.ap()[kt0 * 128:(kt0 + 16) * 128, h * HD:(h + 1) * HD].rearrange("(t p) e -> p t e", p=128), writes=[v.d])

    def load_q(h, j):
        q = Qt[(h * NO + j) % 2]
        P.dma(q.t[0:96, :], QB.ap()[h, :, j * 512:(j + 1) * 512], writes=[q.d])

    load_kv(0)
    load_q(0, 0)
    gu = 0
    for h in range(12):
        if h + 1 < 12:
            load_kv(h + 1)
        k, v = Kh[h % 2], Vh[h % 2]
        for j in range(NO):
            nxt = h * NO + j + 1
            if nxt < 12 * NO:
                load_q(nxt // NO, nxt % NO)
            q = Qt[(h * NO + j) % 2]
            units = key_units(C, 0, j)
            npair = len(units) // 2

            def qk(p):
                s2 = pS[(gu + p) % 3]
                for hf in range(2):
                    kt, mk = units[2 * p + hf]
                    dg = (mk is not None and mk[0] == "d")
                    P.op("pe", lambda e: e.matmul(s2.t[:, hf * 512:(hf + 1) * 512], lhsT=k.t[0:96, kt * 128:(kt + 1) * 128], rhs=q.t[0:96, :], start=True, stop=not dg),
                         [k.d, q.d], [s2.d])
                    if dg:
                        P.op("pe", lambda e: e.matmul(s2.t[:, hf * 512:(hf + 1) * 512], lhsT=C.idb.t[:], rhs=dm.t[:, mk[1], :], start=False, stop=True),
                             [C.idb.d, dm.d], [s2.d])

            qk(0)
            qk(1)
            qk(2)
            for p in range(npair):
                s2, pt = pS[(gu + p) % 3], Pt[(gu + p) % 4]
                mk0, mk1 = units[2 * p][1], units[2 * p + 1][1]
                if mk0 is not None and mk0[0] == "x":
                    assert mk1 == mk0
                    P.op("act", lambda e: e.activation(out=pt.t[:], in_=s2.t[:], func=AF.Exp, bias=cx.t[:, mk0[1]:mk0[1] + 1]), [s2.d, cx.d], [pt.d])
                else:
                    P.op("act", lambda e: e.activation(out=pt.t[:], in_=s2.t[:], func=AF.Exp), [s2.d], [pt.d])
                for hf in range(2):
                    kt = units[2 * p + hf][0]
                    P.op("pe", lambda e: e.matmul(pO.t[:], lhsT=v.t[:, kt, :], rhs=pt.t[:, hf * 512:(hf + 1) * 512], start=(p == 0 and hf == 0), stop=(p == npair - 1 and hf == 1)),
                         [v.d, pt.d], [pO.d])
                if p + 3 < npair:
                    qk(p + 3)
            nkt = npair
            gu += nkt
            o16 = ot[(h * NO + j) % 2]
            osb = Osb[(h * NO + j) % 2]
            P.op("act", lambda e: e.activation(out=osb.t[:], in_=pO.t[:], func=AF.Copy), [pO.d], [osb.d])
            P.op("pe", lambda e: e.matmul(pL.t[0:64, :], lhsT=C.idf.t[:, 64:128], rhs=osb.t[:], start=True, stop=True), [C.idf.d, osb.d], [pL.d])
            P.op("dve", lambda e: e.reciprocal(out=R.t[:], in_=pL.t[0:64, :]), [pL.d], [R.d])
            P.op("dve", lambda e, o16=o16: e.tensor_tensor(out=o16.t[:], in0=osb.t[0:64, :], in1=R.t[:], op=ALU.mult), [osb.d, R.d], [o16.d])
            P.dma(OB.ap()[h, :, j * 512:(j + 1) * 512], o16.t[:], reads=[o16.d], writes=[Dep()])
    phase_end(C)


def attn_out_phase(C, contribs, w_out, x_src, x_dst, ntiles):
    P, sb, ps = C.P, C.sb, C.ps
    phase_begin(C)
    NO = ntiles
    nct = len(contribs)
    Wo = sb("Wo", [128, nct, D], BF16)
    for ci, (_, Kp, row0) in enumerate(contribs):
        st = next_stg(C)
        P.dma(st.t[0:Kp, :], w_out[row0:row0 + Kp, :], writes=[st.d])
        cast_op(C, Wo.t[0:Kp, ci, :], st.t[0:Kp, :], [st.d], [Wo.d])
    ot = [sb("ot%d" % i, [128, nct, 512], BF16) for i in range(2)]
    xres = [sb("xres%d" % i, [128, 4, D], F32) for i in range(2)]
    pA = [ps("pA%d" % i, [128, 512], F32) for i in range(2)]

    def loads(j):
        o_, xr = ot[j % 2], xres[j % 2]
        for ci, (ap_, Kp, _) in enumerate(contribs):
            P.dma(o_.t[0:Kp, ci, :], ap_[:, j * 512:(j + 1) * 512], writes=[o_.d])
        P.dma(xr.t[:], x_src[j * 512:(j + 1) * 512, :].rearrange("(s p) d -> p s d", p=128), writes=[xr.d])

    loads(0)
    for j in range(NO):
        if j + 1 < NO:
            loads(j + 1)
        o_, xr = ot[j % 2], xres[j % 2]
        for s in range(4):
            for nh in range(2):
                pa = pA[(s * 2 + nh) % 2]
                for ci, (_, Kp, _) in enumerate(contribs):
                    P.op("pe", lambda e, ci=ci, Kp=Kp, s=s, nh=nh, pa=pa: e.matmul(pa.t[:], lhsT=o_.t[0:Kp, ci, s * 128:(s + 1) * 128], rhs=Wo.t[0:Kp, ci, nh * 512:(nh + 1) * 512],
                                                                                start=(ci == 0), stop=(ci == nct - 1)),
                         [o_.d, Wo.d], [pa.d])
                P.op("dve", lambda e, s=s, nh=nh, pa=pa: e.tensor_tensor(out=xr.t[:, s, nh * 512:(nh + 1) * 512], in0=pa.t[:], in1=xr.t[:, s, nh * 512:(nh + 1) * 512], op=ALU.add),
                     [pa.d, xr.d], [xr.d])
        P.dma(x_dst[j * 512:(j + 1) * 512, :].rearrange("(s p) d -> p s d", p=128), xr.t[:], reads=[xr.d], writes=[Dep()])
    phase_end(C)


def mlp_phase(C, w_up, w_down, gain, x_src, x_dst, ntiles, final_g=None, Wup=None):
    P, sb, ps = C.P, C.sb, C.ps
    phase_begin(C, nstg=2)
    Wdn = sb("Wdn", [128, 32, D], BF16)
    if Wup is None:
        Wup = sb("Wup", [128, 8, DFF], BF16)
        load_w(C, Wup, w_up, 8, DFF, gain)
    load_w(C, Wdn, w_down, 32, D, None)
    xres = [sb("xres%d" % i, [128, 2, D], F32) for i in range(2)]
    hT = [sb("hT%d" % i, [128, 8, 256], BF16) for i in range(2)]
    uT = sb("uT", [128, 32, 256], BF16)
    r = [sb("r%d" % i, [128, 256], F32) for i in range(3)]
    pT = [ps("pT0", [128, 1024], BF16)]
    pU = [ps("pU%d" % i, [128, 512], F32) for i in range(3)]
    pD = [ps("pD%d" % i, [128, 512], F32) for i in range(2)]
    if final_g is not None:
        gfin = sb("gfin", [128, D], F32)
        P.dma(gfin.t[:], bass.AP(final_g, 0, [[0, 128], [1, D]]), writes=[gfin.d])
        yo = sb("yo", [128, 2, D], F32)

    def loads(T):
        xr = xres[T % 2]
        P.dma(xr.t[:], x_src[T * 256:(T + 1) * 256, :].rearrange("(s p) d -> p s d", p=128), writes=[xr.d])

    loads(0)
    norm_T(C, xres[0], 2, hT[0], pT)
    for T in range(ntiles):
        if T + 1 < ntiles:
            loads(T + 1)
        xr, h = xres[T % 2], hT[T % 2]
        for f in range(32):
            pu, rr = pU[f % 3], r[f % 3]
            for kc in range(8):
                P.op("pe", lambda e, kc=kc, f=f, pu=pu: e.matmul(pu.t[:, 0:256], lhsT=Wup.t[:, kc, f * 128:(f + 1) * 128], rhs=h.t[:, kc, :], start=(kc == 0), stop=(kc == 7)),
                     [Wup.d, h.d], [pu.d])
            P.op("act", lambda e, pu=pu, rr=rr: e.activation(out=rr.t[:], in_=pu.t[:, 0:256], func=AF.Relu), [pu.d], [rr.d])
            P.op("pool", lambda e, f=f, rr=rr: e.tensor_tensor(out=uT.t[:, f, :], in0=rr.t[:], in1=rr.t[:], op=ALU.mult), [rr.d], [uT.d])
        if T + 1 < ntiles:
            norm_T(C, xres[(T + 1) % 2], 2, hT[(T + 1) % 2], pT)
        for s in range(2):
            for nh in range(2):
                pd = pD[(s * 2 + nh) % 2]
                for f in range(32):
                    P.op("pe", lambda e, f=f, s=s, nh=nh, pd=pd: e.matmul(pd.t[:], lhsT=uT.t[:, f, s * 128:(s + 1) * 128], rhs=Wdn.t[:, f, nh * 512:(nh + 1) * 512], start=(f == 0), stop=(f == 31)),
                         [uT.d, Wdn.d], [pd.d])
                P.op("dve", lambda e, s=s, nh=nh, pd=pd: e.tensor_tensor(out=xr.t[:, s, nh * 512:(nh + 1) * 512], in0=pd.t[:], in1=xr.t[:, s, nh * 512:(nh + 1) * 512], op=ALU.add),
                     [pd.d, xr.d], [xr.d])
        dst = x_dst[T * 256:(T + 1) * 256, :].rearrange("(s p) d -> p s d", p=128)
        if final_g is None:
            P.dma(dst, xr.t[:], reads=[xr.d], writes=[Dep()])
        else:
            ss, rs = C.ss, C.rs
            for s in range(2):
                P.op("act", lambda e, s=s: e.activation(out=C.junk.t[:], in_=xr.t[:, s, :], func=AF.Square, accum_out=ss.t[:, s:s + 1]), [xr.d], [C.junk.d, ss.d])
            P.op("act", lambda e: e.activation(out=ss.t[:, 0:2], in_=ss.t[:, 0:2], func=AF.Sqrt, scale=1.0 / D, bias=EPS), [ss.d], [ss.d])
            P.op("dve", lambda e: e.reciprocal(out=rs.t[:, 0:2], in_=ss.t[:, 0:2]), [ss.d], [rs.d])
            for s in range(2):
                P.op("dve", lambda e, s=s: e.scalar_tensor_tensor(out=yo.t[:, s, :], in0=xr.t[:, s, :], scalar=rs.t[:, s:s + 1], in1=gfin.t[:], op0=ALU.mult, op1=ALU.mult),
                     [xr.d, rs.d, gfin.d], [yo.d])
            P.dma(dst, yo.t[:], reads=[yo.d], writes=[Dep()])
    phase_end(C)


def build(S, dbg=False, upto=99):
    nc = bass.Bass("TRN2", target_bir_lowering=False)
    C = Ctx()
    C.nc = nc
    C.P = P = Prog(nc)
    NT = S // 512
    NO = NT // 2
    SO = NO * 512
    NKT = S // 128
    C.S, C.NT, C.NO, C.SO, C.NKT, C.NB = S, NT, NO, SO, NKT, S // 256

    def din(name, shape, dt=F32):
        return nc.dram_tensor(name, list(shape), dt, kind="ExternalInput")

    def dout(name, shape, dt=F32):
        return nc.dram_tensor(name, list(shape), dt, kind="ExternalOutput")

    def dscr(name, shape, dt=BF16):
        if dbg:
            return nc.dram_tensor(name, list(shape), dt, kind="ExternalOutput")
        return nc.dram_tensor(name, list(shape), dt)

    uid = [0]

    def sb(name, shape, dt):
        uid[0] += 1
        return Tile(nc.alloc_sbuf_tensor("%s_%d" % (name, uid[0]), list(shape), dt))

    def ps(name, shape, dt=F32):
        uid[0] += 1
        return Tile(nc.alloc_psum_tensor("%s_%d" % (name, uid[0]), list(shape), dt))

    C.sb, C.ps = sb, ps
    ident_d = din("ident", [128, 128])
    dmask_d = din("dmask", [4, 128, 512])
    cmx_d = din("cmx", [128, 4])
    mem_d = din("mem", [MEM, D])
    ropeK_d = din("ropeK", [2, 128, S])
    ropeQ_d = din("ropeQ", [2, 128, S])
    onehot_d = din("onehot", [32, S])
    gb_d = din("gate_gb", [SO, 32])
    t3_d = din("gate_t3", [SO, 32])
    x_perm = din("x_perm", [S, D])
    W = {}
    for name, shape in [("a_norm_attn", [1, D]), ("a_w_in", [1, D, 3 * SW + MW]), ("a_lambda", [1, 4, HD]), ("a_subln", [1, 128]),
                        ("a_mem_norm", [1, D]), ("a_w_mem_kv", [1, D, 2 * MW]), ("a_w_out", [1, D, D]), ("a_norm_mlp", [1, D]),
                        ("a_w_up", [1, D, DFF]), ("a_w_down", [1, DFF, D]), ("kv_norm", [D]), ("w_kv", [D, 2 * SW]),
                        ("b_norm_attn", [1, D]), ("b_w_in", [1, D, SW + MW]), ("b_mem_norm", [1, D]), ("b_w_mem_kv", [1, D, 2 * MW]),
                        ("b_w_out", [1, D, D]), ("b_norm_mlp", [1, D]), ("b_w_up", [1, D, DFF]), ("b_w_down", [1, DFF, D]), ("final_norm", [D])]:
        W[name] = din(name, shape)
    OUT = dout("out", [SO, D], F32)

    C.idf = sb("idf", [128, 128], F32)
    C.idb = sb("idb", [128, 128], BF16)
    C.onesb = sb("onesb", [128, 128], BF16)
    C.onesf = sb("onesf", [128, 128], F32)
    C.ones1 = sb("ones1", [128, 128], F32)
    C.ss = sb("ss", [128, 4], F32)
    C.rs = sb("rs", [128, 4], F32)
    C.junk = sb("junk", [128, 1024], BF16)
    C.xb = [sb("xb%d" % i, [128, 1024], BF16) for i in range(2)]
    P.dma(C.idf.t[:], ident_d.ap(), writes=[C.idf.d])
    P.op("dve", lambda e: e.tensor_copy(out=C.idb.t[:], in_=C.idf.t[:]), [C.idf.d], [C.idb.d])
    P.op("pool", lambda e: e.memset(C.onesb.t[:], 1.0), [], [C.onesb.d])
    P.op("pool", lambda e: e.memset(C.onesf.t[:], 1.0 / 128.0), [], [C.onesf.d])
    P.op("pool", lambda e: e.memset(C.ones1.t[:], 1.0), [], [C.ones1.d])

    def load_gain(name, ap1d):
        g = sb("g_" + name, [128, 8], F32)
        P.dma(g.t[:], ap1d.rearrange("(c p) -> p c", p=128), writes=[g.d], allow_slow_non_contiguous=True)
        return g

    ga_attn = load_gain("a_attn", W["a_norm_attn"].ap()[0])
    ga_mem = load_gain("a_mem", W["a_mem_norm"].ap()[0])
    ga_mlp = load_gain("a_mlp", W["a_norm_mlp"].ap()[0])
    g_kv = load_gain("kv", W["kv_norm"].ap())
    gb_attn = load_gain("b_attn", W["b_norm_attn"].ap()[0])
    gb_mem = load_gain("b_mem", W["b_mem_norm"].ap()[0])
    gb_mlp = load_gain("b_mlp", W["b_norm_mlp"].ap()[0])
    C.sb_mark = nc.sbuf_base
    C.ps_mark = nc.psum_base

    KA = dscr("KA", [6, 128, S])
    VA = dscr("VA", [S, SW])
    QA = dscr("QA", [6, 128, S])
    QMa = dscr("QMa", [2, 128, S])
    OA = dscr("OA", [6, 128, S])
    OMa = dscr("OMa", [4, 64, S])
    X1a = dscr("X1a", [S, D], F32)
    X1 = dscr("X1", [S, D], F32)
    KB = dscr("KB", [6, 128, S])
    VB = dscr("VB", [S, SW])
    KM = dscr("KM", [6, 128, 32], F32)
    QB = dscr("QB", [12, 96, SO])
    QMb = dscr("QMb", [2, 128, SO])
    OB = dscr("OB", [12, 64, SO])
    OMb = dscr("OMb", [4, 64, SO])
    X2a = dscr("X2a", [SO, D], F32)

    w_in = W["a_w_in"].ap()[0]
    proj_phase(C, x_perm.ap(), NT, ga_attn,
               fm=[dict(w=w_in[:, SW + 128 * h: SW + 128 * (h + 1)], rope=0, out=KA.ap()[h]) for h in range(6)]
               + [dict(w=w_in[:, 128 * h:128 * (h + 1)], rope=1, out=QA.ap()[h]) for h in range(6)]
               + [dict(w=w_in[:, 3 * SW + 128 * c:3 * SW + 128 * (c + 1)], rope=None, scale=0.125, out=QMa.ap()[c]) for c in range(2)],
               rope_d=[ropeK_d, ropeQ_d], tm=dict(w=w_in[:, 2 * SW:3 * SW], out=VA.ap()))
    if upto >= 2:
        mem_phase(C, mem_d, W["a_w_mem_kv"].ap()[0], ga_mem, QMa, OMa, NT)
    base_mark = C.sb_mark
    WupA = None
    if upto >= 3:
        WupA = sb("WupA", [128, 8, DFF], BF16)
        C.sb_mark = nc.sbuf_base
        diff_attn_phase(C, KA, VA, QA, OA, W["a_lambda"], W["a_subln"], dmask_d, cmx_d,
                        pre=lambda: load_w(C, WupA, W["a_w_up"].ap()[0], 8, DFF, ga_mlp))
    if upto >= 4:
        attn_out_phase(C, [(OA.ap()[h], 128, 128 * h) for h in range(6)] + [(OMa.ap()[m], 64, SW + 64 * m) for m in range(4)],
                       W["a_w_out"].ap()[0], x_perm.ap(), X1a.ap(), NT)
    if upto >= 5:
        mlp_phase(C, W["a_w_up"].ap()[0], W["a_w_down"].ap()[0], ga_mlp, X1a.ap(), X1.ap(), S // 256, final_g=None, Wup=WupA)
    C.sb_mark = base_mark
    nc.sbuf_base = base_mark
    if upto >= 6:
        w_kv = W["w_kv"].ap()
        proj_phase(C, X1.ap(), NT, g_kv,
                   fm=[dict(w=w_kv[:, 128 * c:128 * (c + 1)], rope=0, out=KB.ap()[c], kmean=KM.ap()[c]) for c in range(6)],
                   rope_d=[ropeK_d], tm=dict(w=w_kv[:, SW:2 * SW], out=VB.ap()))
    if upto >= 7:
        w_in = W["b_w_in"].ap()[0]
        proj_phase(C, X1.ap(), NO, gb_attn,
                   fm=[dict(w=w_in[:, 128 * c:128 * (c + 1)], rope=0, gate=c) for c in range(6)]
                   + [dict(w=w_in[:, SW + 128 * c:SW + 128 * (c + 1)], rope=None, scale=0.125, out=QMb.ap()[c]) for c in range(2)],
                   rope_d=[ropeQ_d], tm=None, gate=dict(km=KM, gb=gb_d, t3=t3_d, QB=QB))
    if upto >= 8:
        mem_phase(C, mem_d, W["b_w_mem_kv"].ap()[0], gb_mem, QMb, OMb, NO)
    WupB = None
    if upto >= 9:
        WupB = sb("WupB", [128, 8, DFF], BF16)
        C.sb_mark = nc.sbuf_base
        moba_attn_phase(C, KB, VB, QB, OB, onehot_d, dmask_d, cmx_d,
                        pre=lambda: load_w(C, WupB, W["b_w_up"].ap()[0], 8, DFF, gb_mlp))
    if upto >= 10:
        attn_out_phase(C, [(OB.ap()[h], 64, 64 * h) for h in range(12)] + [(OMb.ap()[m], 64, SW + 64 * m) for m in range(4)],
                       W["b_w_out"].ap()[0], X1.ap(), X2a.ap(), NO)
    if upto >= 11:
        mlp_phase(C, W["b_w_up"].ap()[0], W["b_w_down"].ap()[0], gb_mlp, X2a.ap(), OUT.ap(), SO // 256, final_g=W["final_norm"], Wup=WupB)

    P.barrier()
    P.emit()
    return nc


def rope_table(pos, scale=1.0):
    half = HD // 2
    inv = (1.0 / (10000.0 ** (np.arange(half, dtype=np.float32) / np.float32(half)))).astype(np.float32)
    ang = pos.astype(np.float32)[:, None] * inv[None, :]
    cos = np.cos(ang).astype(np.float32)
    sin = np.sin(ang).astype(np.float32)
    p = np.arange(128)
    sign = np.where((p % 64) < 32, -1.0, 1.0).astype(np.float32)
    tab = np.empty((2, 128, len(pos)), np.float32)
    tab[0] = cos.T[p % 32, :] * np.float32(scale)
    tab[1] = sin.T[p % 32, :] * sign[:, None] * np.float32(scale)
    return tab


def core_tables(S, r):
    NT = S // 512
    half = NT // 2
    groups = [own_tiles(NT, r), own_tiles(NT, 1 - r)]
    perm_tiles = groups[0] + groups[1]
    pos = np.concatenate([np.arange(t * 512, (t + 1) * 512) for t in perm_tiles])
    pos_own = pos[:half * 512]
    cmx = np.zeros((128, 4), np.float32)
    for gq in range(2):
        for par in range(2):
            vals = set()
            for j in range(par, half, 2):
                vals.add(1.0 if groups[1 - gq][j] < groups[gq][j] else 0.0)
            assert len(vals) == 1
            cmx[:, gq * 2 + par] = vals.pop()
    kk = np.arange(128)[:, None]
    q = np.arange(512)[None, :]
    dmask = np.stack([(128 * i + kk <= q).astype(np.float32) for i in range(4)])
    blk = (pos[::256] // 256)
    nblk = len(blk)
    cur = pos_own // 256
    gb = np.full((len(pos_own), 32), -1e30, np.float32)
    t3 = np.full((len(pos_own), 32), -1.0, np.float32)
    gb[:, :nblk] = np.where(blk[None, :] < cur[:, None], 0.0, -1e30)
    t3[:, :nblk] = np.where(blk[None, :] == cur[:, None], 0.0, -1.0)
    onehot = np.zeros((32, S), np.float32)
    onehot[np.arange(S) // 256, np.arange(S)] = 1.0
    return dict(pos=pos, pos_own=pos_own, cmx=cmx, dmask=dmask, gate_gb=gb, gate_t3=t3, onehot=onehot,
                ropeK=rope_table(pos, 1.0), ropeQ=rope_table(pos, 0.125))


_cache = {}
W_NAMES = ["a_norm_attn", "a_w_in", "a_lambda", "a_subln", "a_mem_norm", "a_w_mem_kv", "a_w_out", "a_norm_mlp", "a_w_up", "a_w_down", "kv_norm", "w_kv",
           "b_norm_attn", "b_w_in", "b_mem_norm", "b_w_mem_kv", "b_w_out", "b_norm_mlp", "b_w_up", "b_w_down", "final_norm"]


def get_prog(S):
    if S not in _cache:
        _cache[S] = build(S)
    return _cache[S]


def make_in_maps(inp, S, B):
    x = np.ascontiguousarray(inp["x"], dtype=np.float32)
    mem = np.ascontiguousarray(inp["mem"], dtype=np.float32)
    ident = np.eye(128, dtype=np.float32)
    tabs = [core_tables(S, r) for r in range(2)]
    ws = {k: np.ascontiguousarray(inp[k], dtype=np.float32) for k in W_NAMES}
    maps = []
    for c in range(2 * B):
        b, r = c // 2, c % 2
        tb = tabs[r]
        m = dict(ws)
        m.update(ident=ident, dmask=tb["dmask"], cmx=tb["cmx"], mem=mem[b], ropeK=tb["ropeK"], ropeQ=tb["ropeQ"], onehot=tb["onehot"],
                 gate_gb=tb["gate_gb"], gate_t3=tb["gate_t3"], x_perm=np.ascontiguousarray(x[b][tb["pos"]]))
        maps.append(m)
    return maps, tabs


def run_model(inp, S, B, nc=None):
    maps, tabs = make_in_maps(inp, S, B)
    if nc is None:
        nc = get_prog(S)
    res = run_bass_kernel_spmd(nc, maps, core_ids=list(range(2 * B))).results
    out = np.empty((B, S, D), np.float32)
    for c in range(2 * B):
        out[c // 2][tabs[c % 2]["pos_own"]] = res[c]["out"]
    return out, res


def kernel(**inputs):
    x = inputs["x"]
    B, S, _ = x.shape
    out, _ = run_model(inputs, S, B)
    return out
```

```python
import math
import numpy as np
import concourse.bass as bass
import concourse.mybir as mybir
from concourse.bass_utils import run_bass_kernel_spmd

F32 = mybir.dt.float32
BF16 = mybir.dt.bfloat16
AF = mybir.ActivationFunctionType
ALU = mybir.AluOpType
AX = mybir.AxisListType

D = 1024
HD = 64
SW = 768
MW = 256
DFF = 4096
MEM = 256
EPS = 1e-6
LAM_INIT = 0.8 - 0.6 * math.exp(-0.3 * 0)
NEGB = 30000.0

ENGS = ["pe", "act", "dve", "pool", "sp"]


class Dep:
    __slots__ = ("w", "r")

    def __init__(self):
        self.w = None
        self.r = {}


class Tile:
    __slots__ = ("t", "d")

    def __init__(self, t):
        self.t = t
        self.d = Dep()


class _Rec:
    def __getattr__(self, name):
        def f(*a, **k):
            self.call = (name, a, k)
            return self
        return f


class Prog:
    def __init__(self, nc, n_dma_sems=40, self_sync=("act", "dve", "pool")):
        self.nc = nc
        self.ins = {e: [] for e in ENGS}
        self.sems = {}
        for e in ENGS:
            self.sems[("e", e)] = nc.alloc_semaphore("s_" + e)
        self.nd = n_dma_sems
        for i in range(n_dma_sems):
            self.sems[("d", i)] = nc.alloc_semaphore("d%d" % i)
        self.cnt = {k: 0 for k in self.sems}
        self.dnext = 0
        self.waited = {}
        self.self_sync = set(self_sync)
        self.nops = 0

    def _wait(self, eng, k, v):
        if self.waited.get((eng, k), 0) >= v:
            return
        self.waited[(eng, k)] = v
        self.ins[eng].append(("wait", k, v))

    def _need(self, eng, deps_r, deps_w):
        need = {}
        for d in deps_r:
            if d.w is not None:
                k, v = d.w
                if need.get(k, 0) < v:
                    need[k] = v
        for d in deps_w:
            if d.w is not None:
                k, v = d.w
                if need.get(k, 0) < v:
                    need[k] = v
            for k, v in d.r.items():
                if need.get(k, 0) < v:
                    need[k] = v
        for k, v in need.items():
            if k == ("e", eng) and eng not in self.self_sync:
                continue
            self._wait(eng, k, v)

    def op(self, eng, fn, reads=(), writes=()):
        self._need(eng, reads, writes)
        k = ("e", eng)
        self.cnt[k] += 1
        v = self.cnt[k]
        rec = _Rec()
        fn(rec)
        self.ins[eng].append(("op", rec.call, k, 1))
        for d in reads:
            d.r[k] = v
        for d in writes:
            d.w = (k, v)
            d.r = {}
        self.nops += 1

    def dma(self, out, in_, reads=(), writes=(), q="sp", **kw):
        i = self.dnext
        self.dnext = (self.dnext + 1) % self.nd
        k = ("d", i)
        prev = self.cnt[k]
        if prev > 0:
            self._wait(q, k, prev)
        self._need(q, reads, writes)
        self.cnt[k] += 16
        v = self.cnt[k]
        kw = dict(kw)
        kw.update(out=out, in_=in_)
        self.ins[q].append(("op", ("dma_start", (), kw), k, 16))
        for d in reads:
            d.r[k] = v
        for d in writes:
            d.w = (k, v)
            d.r = {}
        self.nops += 1

    def collective(self, kind, op, groups, ins, outs):
        k = ("c", 0)
        if k not in self.sems:
            self.sems[k] = self.nc.alloc_semaphore("cc_sem")
            self.cnt[k] = 0
        self.cnt[k] += 16
        self.ins["pool"].append(("op", ("collective_compute", (kind, op), dict(replica_groups=groups, ins=ins, outs=outs)), k, 16))
        self.nops += 1

    def barrier(self, engs=ENGS):
        snap = dict(self.cnt)
        for eng in engs:
            for k, v in snap.items():
                if v <= 0:
                    continue
                if k == ("e", eng) and eng not in self.self_sync:
                    continue
                self._wait(eng, k, v)

    def emit(self):
        nc = self.nc
        sems = self.sems
        ins = self.ins
        with nc.Block() as block:
            def run(e, lst):
                pend = []
                for it in lst:
                    if it[0] == "wait":
                        pend.append(it)
                    else:
                        for w in pend[:-1]:
                            e.wait_ge(sems[w[1]], w[2])
                        r = getattr(e, it[1][0])(*it[1][1], **it[1][2])
                        if pend:
                            r = r._wait_ge(sems[pend[-1][1]], pend[-1][2])
                        r.then_inc(sems[it[2]], it[3])
                        pend = []
                for w in pend:
                    e.wait_ge(sems[w[1]], w[2])

            @block.tensor
            def _(e):
                run(e, ins["pe"])

            @block.scalar
            def _(e):
                run(e, ins["act"])

            @block.vector
            def _(e):
                run(e, ins["dve"])

            @block.gpsimd
            def _(e):
                run(e, ins["pool"])

            @block.sync
            def _(e):
                run(e, ins["sp"])


def bcast_ap(ap, pos, count):
    lst = [list(x) for x in ap.ap]
    lst.insert(pos, [0, count])
    return bass.AP(ap.tensor, ap.offset, lst)


def own_tiles(NT, r):
    res = []
    for g in range(NT // 4):
        res += [4 * g, 4 * g + 3] if r == 0 else [4 * g + 1, 4 * g + 2]
    return res


def tmax_of(j):
    return 4 * (j // 2) + (1 if j % 2 == 0 else 3)


class Ctx:
    pass


def next_stg(C):
    st = C.stg[C.stg_i % len(C.stg)]
    C.stg_i += 1
    return st


def phase_begin(C, nstg=3):
    C.stg = [C.sb("stg%d" % i, [128, 1024], F32) for i in range(nstg)]
    C.stg_i = 0


def phase_end(C):
    C.P.barrier()
    C.nc.sbuf_base = C.sb_mark
    C.nc.psum_base = C.ps_mark


_rr = [0]


def cast_op(C, o, i, rd, wr, gcol=None, engs=("act", "dve", "pool")):
    P = C.P
    eng = engs[_rr[0] % len(engs)]
    _rr[0] += 1
    if gcol is None:
        if eng == "act":
            P.op(eng, lambda e: e.activation(out=o, in_=i, func=AF.Copy), rd, wr)
        else:
            P.op(eng, lambda e: e.tensor_copy(out=o, in_=i), rd, wr)
    else:
        if eng == "act":
            P.op(eng, lambda e: e.activation(out=o, in_=i, func=AF.Copy, scale=gcol), rd, wr)
        else:
            P.op(eng, lambda e: e.tensor_scalar(out=o, in0=i, scalar1=gcol, scalar2=None, op0=ALU.mult), rd, wr)


def load_w(C, dst, src, KC, N, gain=None):
    P = C.P
    for kc in range(KC):
        for n0 in range(0, N, 1024):
            n1 = min(N, n0 + 1024)
            w = n1 - n0
            st = next_stg(C)
            P.dma(st.t[:, 0:w], src[kc * 128:(kc + 1) * 128, n0:n1], writes=[st.d])
            rd = [st.d] + ([gain.d] if gain is not None else [])
            cast_op(C, dst.t[:, kc, n0:n1], st.t[:, 0:w], rd, [dst.d], None if gain is None else gain.t[:, kc:kc + 1])


def load_w_chunk(C, Wf, ci, src, gain, Wsw):
    P = C.P
    st = next_stg(C)
    sv = st.t[:, 0:1024].rearrange("p (c n) -> p c n", n=128)
    P.dma(sv, src.rearrange("(c p) n -> p c n", p=128), writes=[st.d])
    for kc in range(8):
        gcol = gain.t[:, kc:kc + 1]
        cast_op(C, Wf.t[:, kc, ci * 128:(ci + 1) * 128], sv[:, kc, :], [st.d, gain.d], [Wf.d], gcol)
        if Wsw is not None:
            for half in range(2):
                o2 = Wsw.t[:, kc, ci * 128:(ci + 1) * 128].rearrange("p (b t i) -> p b t i", t=2, i=32)[:, :, half, :]
                i2 = sv[:, kc, :].rearrange("p (b t i) -> p b t i", t=2, i=32)[:, :, 1 - half, :]
                cast_op(C, o2, i2, [st.d, gain.d], [Wsw.d], gcol, engs=("dve", "pool"))


def norm_T(C, xres, nsub, hT, pT):
    P = C.P
    ss, rs = C.ss, C.rs
    for s in range(nsub):
        P.op("act", lambda e, s=s: e.activation(out=C.junk.t[:], in_=xres.t[:, s, :], func=AF.Square, accum_out=ss.t[:, s:s + 1]),
             [xres.d], [C.junk.d, ss.d])
    P.op("act", lambda e: e.activation(out=ss.t[:, 0:nsub], in_=ss.t[:, 0:nsub], func=AF.Sqrt, scale=1.0 / D, bias=EPS), [ss.d], [ss.d])
    P.op("dve", lambda e: e.reciprocal(out=rs.t[:, 0:nsub], in_=ss.t[:, 0:nsub]), [ss.d], [rs.d])
    for s in range(nsub):
        xb = C.xb[s % 2]
        pt = pT[s % len(pT)]
        P.op("dve", lambda e, s=s, xb=xb: e.tensor_scalar(out=xb.t[:], in0=xres.t[:, s, :], scalar1=rs.t[:, s:s + 1], scalar2=None, op0=ALU.mult),
             [xres.d, rs.d], [xb.d])
        for kc in range(8):
            P.op("pe", lambda e, kc=kc, xb=xb, pt=pt: e.transpose(out=pt.t[:, kc * 128:(kc + 1) * 128], in_=xb.t[:, kc * 128:(kc + 1) * 128], identity=C.idb.t[:]),
                 [xb.d, C.idb.d], [pt.d])
        P.op("act", lambda e, s=s, pt=pt: e.activation(out=hT.t[:, :, s * 128:(s + 1) * 128], in_=pt.t[:].rearrange("p (c t) -> p c t", c=8), func=AF.Copy),
             [pt.d], [hT.d])


def proj_phase(C, x_src, ntiles, gain, fm, rope_d, tm, gate=None):
    P, sb, ps = C.P, C.sb, C.ps
    phase_begin(C)
    nfm = len(fm)
    nrope = sum(1 for f in fm if f["rope"] is not None)
    rope_ds = rope_d if isinstance(rope_d, (list, tuple)) else [rope_d]
    nrt = len(rope_ds)
    Wf = sb("Wf", [128, 8, 128 * nfm], BF16)
    Wsw = sb("Wsw", [128, 8, 128 * nrope], BF16) if nrope else None
    for ci, f in enumerate(fm):
        assert (f["rope"] is not None) == (ci < nrope)
        load_w_chunk(C, Wf, ci, f["w"], gain, Wsw if f["rope"] is not None else None)
    Wv = None
    if tm is not None:
        Wv = sb("Wv", [128, 8, SW], BF16)
        load_w(C, Wv, tm["w"], 8, SW, gain)
    xres = [sb("xres%d" % i, [128, 4, D], F32) for i in range(2)]
    hT = [sb("hT%d" % i, [128, 8, 512], BF16) for i in range(2)]
    rope = [sb("rope%d" % i, [128, nrt, 2, 512], F32) for i in range(2)]
    t1 = [sb("t1_%d" % i, [128, 512], F32) for i in range(2)]
    t2 = [sb("t2_%d" % i, [128, 512], F32) for i in range(2)]
    kf = [sb("kf%d" % i, [128, 512], F32) for i in range(2)]
    kb = [sb("kb%d" % i, [128, 512], BF16) for i in range(3)]
    nab = 1 if gate is not None else 2
    pT = [ps("pT%d" % i, [128, 1024], BF16) for i in range(2)]
    pA = [ps("pA%d" % i, [128, 512], F32) for i in range(nab)]
    pB = [ps("pB%d" % i, [128, 512], F32) for i in range(nab)]
    pV = [ps("pV%d" % i, [128, 512], F32) for i in range(2)] if tm is not None else None
    vsb = [sb("vsb%d" % i, [128, 4, SW], BF16) for i in range(2)] if tm is not None else None
    kms = None
    if any("kmean" in f for f in fm):
        kms = sb("kms", [128, 6, 32], F32)
        P.op("pool", lambda e: e.memset(kms.t[:], 0.0), [], [kms.d])
    if gate is not None:
        kmt = sb("kmt", [128, 6, 32], F32)
        P.dma(kmt.t[:], gate["km"].ap().rearrange("c p n -> p c n"), writes=[kmt.d])
        kmz = sb("kmz", [128, 12, 32], F32)
        P.op("pool", lambda e: e.memset(kmz.t[:], 0.0), [], [kmz.d])
        for hd in range(12):
            c6, po = hd // 2, (hd % 2) * 64
            P.op("dve", lambda e, hd=hd, c6=c6, po=po: e.tensor_copy(out=kmz.t[po:po + 64, hd, :], in_=kmt.t[po:po + 64, c6, :]), [kmt.d, kmz.d], [kmz.d])
        gbt = [sb("gbt%d" % i, [128, 4, 32], F32) for i in range(2)]
        t3t = [sb("t3t%d" % i, [128, 4, 32], F32) for i in range(2)]
        q32 = sb("q32", [128, 6, 512], F32)
        G = sb("G", [128, 12, 32], F32)
        top8 = sb("top8", [128, 12, 8], F32)
        thr = sb("thr", [128, 12], F32)
        fin = sb("fin", [128, 12, 32], F32)
        biasT = [sb("biasT%d" % i, [128, 512], BF16) for i in range(2)]
        pG = ps("pG", [128, 512], F32)
        pGT = [ps("pGT%d" % i, [128, 512], F32) for i in range(3)]
    ctr = dict(ab=0, kb=0)

    def loads(T):
        xr = xres[T % 2]
        P.dma(xr.t[:], x_src[T * 512:(T + 1) * 512, :].rearrange("(s p) d -> p s d", p=128), writes=[xr.d])
        if nrope:
            rp = rope[T % 2]
            for ri, rd_ in enumerate(rope_ds):
                P.dma(rp.t[:, ri], rd_.ap()[:, :, T * 512:(T + 1) * 512].rearrange("c p t -> p c t"), writes=[rp.d])
        if gate is not None:
            P.dma(gbt[T % 2].t[:], gate["gb"].ap()[T * 512:(T + 1) * 512, :].rearrange("(s p) n -> p s n", p=128), writes=[gbt[T % 2].d])
            P.dma(t3t[T % 2].t[:], gate["t3"].ap()[T * 512:(T + 1) * 512, :].rearrange("(s p) n -> p s n", p=128), writes=[t3t[T % 2].d])

    loads(0)
    for T in range(ntiles):
        if T + 1 < ntiles:
            loads(T + 1)
        xr, h, rp = xres[T % 2], hT[T % 2], rope[T % 2]
        tsl = slice(T * 512, (T + 1) * 512)
        norm_T(C, xr, 4, h, pT)
        for ci, f in enumerate(fm):
            a = pA[ctr["ab"] % nab]
            for kc in range(8):
                P.op("pe", lambda e, kc=kc, ci=ci, a=a, h=h: e.matmul(a.t[:], lhsT=Wf.t[:, kc, ci * 128:(ci + 1) * 128], rhs=h.t[:, kc, :], start=(kc == 0), stop=(kc == 7)),
                     [Wf.d, h.d], [a.d])
            k16 = kb[ctr["kb"] % 3]
            ctr["kb"] += 1
            if f["rope"] is not None:
                ri = f["rope"]
                b = pB[ctr["ab"] % nab]
                for kc in range(8):
                    P.op("pe", lambda e, kc=kc, ci=ci, b=b, h=h: e.matmul(b.t[:], lhsT=Wsw.t[:, kc, ci * 128:(ci + 1) * 128], rhs=h.t[:, kc, :], start=(kc == 0), stop=(kc == 7)),
                         [Wsw.d, h.d], [b.d])
                u1, u2 = t1[ctr["ab"] % 2], t2[ctr["ab"] % 2]
                P.op("dve", lambda e, a=a, u1=u1, rp=rp, ri=ri: e.tensor_tensor(out=u1.t[:], in0=a.t[:], in1=rp.t[:, ri, 0, :], op=ALU.mult), [a.d, rp.d], [u1.d])
                P.op("dve", lambda e, b=b, u2=u2, rp=rp, ri=ri: e.tensor_tensor(out=u2.t[:], in0=b.t[:], in1=rp.t[:, ri, 1, :], op=ALU.mult), [b.d, rp.d], [u2.d])
                if "gate" in f:
                    c6 = f["gate"]
                    P.op("pool", lambda e, u1=u1, u2=u2, c6=c6: e.tensor_tensor(out=q32.t[:, c6, :], in0=u1.t[:], in1=u2.t[:], op=ALU.add), [u1.d, u2.d], [q32.d])
                    P.op("act", lambda e, k16=k16, c6=c6: e.activation(out=k16.t[:], in_=q32.t[:, c6, :], func=AF.Copy), [q32.d], [k16.d])
                    for hh in range(2):
                        P.dma(gate["QB"].ap()[2 * c6 + hh, 0:64, tsl], k16.t[hh * 64:(hh + 1) * 64, :], reads=[k16.d], writes=[Dep()])
                else:
                    kk = kf[ctr["ab"] % 2]
                    P.op("pool", lambda e, u1=u1, u2=u2, kk=kk: e.tensor_tensor(out=kk.t[:], in0=u1.t[:], in1=u2.t[:], op=ALU.add), [u1.d, u2.d], [kk.d])
                    P.op("act", lambda e, k16=k16, kk=kk: e.activation(out=k16.t[:], in_=kk.t[:], func=AF.Copy), [kk.d], [k16.d])
                    P.dma(f["out"][:, tsl], k16.t[:], reads=[k16.d], writes=[Dep()])
                    if "kmean" in f:
                        P.op("dve", lambda e, kk=kk, ci=ci, T=T: e.tensor_reduce(out=kms.t[:, ci, 2 * T:2 * T + 2], in_=kk.t[:].rearrange("p (b k) -> p b k", k=256), axis=AX.X, op=ALU.add),
                             [kk.d], [kms.d])
            else:
                sc = f.get("scale", 1.0)
                P.op("act", lambda e, k16=k16, a=a, sc=sc: e.activation(out=k16.t[:], in_=a.t[:], func=AF.Copy, scale=sc), [a.d], [k16.d])
                P.dma(f["out"][:, tsl], k16.t[:], reads=[k16.d], writes=[Dep()])
            ctr["ab"] += 1
        if tm is not None:
            vs = vsb[T % 2]
            for s in range(4):
                pv0, pv1 = pV[0], pV[1]
                for kc in range(8):
                    P.op("pe", lambda e, kc=kc, s=s, h=h, pv0=pv0: e.matmul(pv0.t[:], lhsT=h.t[:, kc, s * 128:(s + 1) * 128], rhs=Wv.t[:, kc, 0:512], start=(kc == 0), stop=(kc == 7)),
                         [Wv.d, h.d], [pv0.d])
                for kc in range(8):
                    P.op("pe", lambda e, kc=kc, s=s, h=h, pv1=pv1: e.matmul(pv1.t[:, 0:256], lhsT=h.t[:, kc, s * 128:(s + 1) * 128], rhs=Wv.t[:, kc, 512:768], start=(kc == 0), stop=(kc == 7)),
                         [Wv.d, h.d], [pv1.d])
                P.op("act", lambda e, s=s, vs=vs, pv0=pv0: e.activation(out=vs.t[:, s, 0:512], in_=pv0.t[:], func=AF.Copy), [pv0.d], [vs.d])
                P.op("dve", lambda e, s=s, vs=vs, pv1=pv1: e.tensor_copy(out=vs.t[:, s, 512:768], in_=pv1.t[:, 0:256]), [pv1.d], [vs.d])
            P.dma(tm["out"][tsl, :].rearrange("(s p) e -> p s e", p=128), vs.t[:], reads=[vs.d], writes=[Dep()])
        if gate is not None:
            gbs, t3s = gbt[T % 2], t3t[T % 2]
            for s in range(4):
                for hd in range(12):
                    c6, po = hd // 2, (hd % 2) * 64
                    P.op("pe", lambda e, s=s, hd=hd, c6=c6, po=po: e.matmul(pG.t[:, hd * 32:(hd + 1) * 32], lhsT=q32.t[:, c6, s * 128:(s + 1) * 128],
                                                                          rhs=kmz.t[:, hd, :], start=True, stop=True),
                         [q32.d, kmz.d], [pG.d])
                P.op("dve", lambda e, s=s, gbs=gbs: e.tensor_tensor(out=G.t[:], in0=pG.t[:, 0:384].rearrange("p (h n) -> p h n", n=32),
                                                                  in1=bcast_ap(gbs.t[:, s, :], 1, 12), op=ALU.add), [pG.d, gbs.d], [G.d])
                for hd in range(12):
                    P.op("dve", lambda e, hd=hd: e.max(out=top8.t[:, hd, :], in_=G.t[:, hd, :]), [G.d], [top8.d])
                P.op("dve", lambda e: e.tensor_scalar(out=thr.t[:], in0=top8.t[:, :, 2], scalar1=-1e29, scalar2=None, op0=ALU.max), [top8.d], [thr.d])
                P.op("dve", lambda e: e.tensor_tensor(out=fin.t[:], in0=G.t[:], in1=bcast_ap(thr.t[:], 2, 32), op=ALU.is_ge), [G.d, thr.d], [fin.d])
                P.op("dve", lambda e, s=s, t3s=t3s: e.tensor_tensor(out=fin.t[:], in0=fin.t[:], in1=bcast_ap(t3s.t[:, s, :], 1, 12), op=ALU.add), [fin.d, t3s.d], [fin.d])
                for grp in range(3):
                    P.op("pe", lambda e, s=s, grp=grp: e.matmul(pGT[grp].t[:, s * 128:(s + 1) * 128],
                                                               lhsT=fin.t[:, 4 * grp:4 * grp + 4, :].rearrange("p h n -> p (h n)"), rhs=C.idf.t[:], start=True, stop=True),
                         [fin.d, C.idf.d], [pGT[grp].d])
            for grp in range(3):
                bt = biasT[grp % 2]
                P.op("act", lambda e, grp=grp, bt=bt: e.activation(out=bt.t[:], in_=pGT[grp].t[:], func=AF.Copy, scale=NEGB), [pGT[grp].d], [bt.d])
                for hl in range(4):
                    P.dma(gate["QB"].ap()[4 * grp + hl, 64:96, tsl], bt.t[hl * 32:(hl + 1) * 32, :], reads=[bt.d], writes=[Dep()])
    if kms is not None:
        P.op("dve", lambda e: e.tensor_scalar(out=kms.t[:], in0=kms.t[:], scalar1=1.0 / 256.0, scalar2=None, op0=ALU.mult), [kms.d], [kms.d])
        for ci, f in enumerate(fm):
            if "kmean" in f:
                P.dma(f["kmean"], kms.t[:, ci, :], reads=[kms.d], writes=[Dep()])
    phase_end(C)


def mem_phase(C, mem_d, w, gain, QM, OM, ntiles):
    P, sb, ps = C.P, C.sb, C.ps
    phase_begin(C)
    NO = ntiles
    Wm = sb("Wm", [128, 8, 2 * MW], BF16)
    load_w(C, Wm, w, 8, 2 * MW, gain)
    xm = sb("xm", [128, 2, D], F32)
    P.dma(xm.t[:], mem_d.ap().rearrange("(s p) d -> p s d", p=128), writes=[xm.d])
    mT = sb("mT", [128, 8, 256], BF16)
    pT = [ps("pT0", [128, 1024], BF16)]
    norm_T(C, xm, 2, mT, pT)
    MKT = sb("MKT", [128, 2, 256], BF16)
    MV = sb("MV", [128, 2, 256], BF16)
    pa = ps("pa", [128, 512], F32)
    for c in range(2):
        for kc in range(8):
            P.op("pe", lambda e, kc=kc, c=c: e.matmul(pa.t[:, 0:256], lhsT=Wm.t[:, kc, c * 128:(c + 1) * 128], rhs=mT.t[:, kc, :], start=(kc == 0), stop=(kc == 7)),
                 [Wm.d, mT.d], [pa.d])
        P.op("act", lambda e, c=c: e.activation(out=MKT.t[:, c, :], in_=pa.t[:, 0:256], func=AF.Copy), [pa.d], [MKT.d])
    for mt in range(2):
        for kc in range(8):
            P.op("pe", lambda e, kc=kc, mt=mt: e.matmul(pa.t[:, 0:256], lhsT=mT.t[:, kc, mt * 128:(mt + 1) * 128], rhs=Wm.t[:, kc, 256:512], start=(kc == 0), stop=(kc == 7)),
                 [Wm.d, mT.d], [pa.d])
        P.op("act", lambda e, mt=mt: e.activation(out=MV.t[:, mt, :], in_=pa.t[:, 0:256], func=AF.Copy), [pa.d], [MV.d])
    qm = [sb("qm%d" % i, [128, 2, 512], BF16) for i in range(2)]
    Pt = [sb("Pt%d" % i, [128, 512], BF16) for i in range(3)]
    Rs = [sb("R%d" % i, [64, 512], F32) for i in range(2)]
    om = [sb("om%d" % i, [64, 512], BF16) for i in range(2)]
    pS = [ps("pS%d" % i, [128, 512], F32) for i in range(2)]
    pOs = [ps("pO%d" % i, [128, 512], F32) for i in range(2)]
    pLs = [ps("pL%d" % i, [128, 512], F32) for i in range(2)]
    u = 0
    for j in range(NO):
        q = qm[j % 2]
        P.dma(q.t[:], QM.ap()[:, :, j * 512:(j + 1) * 512].rearrange("c p t -> p c t"), writes=[q.d])
        for mh in range(4):
            c, po = mh // 2, (mh % 2) * 64
            pO, pL, R = pOs[(j * 4 + mh) % 2], pLs[(j * 4 + mh) % 2], Rs[(j * 4 + mh) % 2]
            for mt in range(2):
                s_, pt = pS[u % 2], Pt[u % 3]
                u += 1
                P.op("pe", lambda e, s_=s_, q=q, c=c, po=po, mt=mt: e.matmul(s_.t[:], lhsT=MKT.t[po:po + 64, c, mt * 128:(mt + 1) * 128], rhs=q.t[po:po + 64, c, :], start=True, stop=True),
                     [MKT.d, q.d], [s_.d])
                P.op("act", lambda e, s_=s_, pt=pt: e.activation(out=pt.t[:], in_=s_.t[:], func=AF.Exp), [s_.d], [pt.d])
                P.op("pe", lambda e, pt=pt, mt=mt, mh=mh: e.matmul(pO.t[0:64, :], lhsT=MV.t[:, mt, mh * 64:(mh + 1) * 64], rhs=pt.t[:], start=(mt == 0), stop=(mt == 1)),
                     [MV.d, pt.d], [pO.d])
                P.op("pe", lambda e, pt=pt, mt=mt: e.matmul(pL.t[0:64, :], lhsT=C.onesb.t[:, 0:64], rhs=pt.t[:], start=(mt == 0), stop=(mt == 1)),
                     [C.onesb.d, pt.d], [pL.d])
            o_ = om[(j * 4 + mh) % 2]
            P.op("dve", lambda e: e.reciprocal(out=R.t[:], in_=pL.t[0:64, :]), [pL.d], [R.d])
            P.op("dve", lambda e, o_=o_: e.tensor_tensor(out=o_.t[:], in0=pO.t[0:64, :], in1=R.t[:], op=ALU.mult), [pO.d, R.d], [o_.d])
            P.dma(OM.ap()[mh, :, j * 512:(j + 1) * 512], o_.t[:], reads=[o_.d], writes=[Dep()])
    phase_end(C)


def load_cmask(C, dmask_d, cmx_d):
    P = C.P
    dm = C.sb("dmask_sb", [128, 4, 512], BF16)
    for i in range(4):
        st = next_stg(C)
        P.dma(st.t[:, 0:512], dmask_d.ap()[i], writes=[st.d])
        P.op("dve", lambda e, st=st, i=i: e.tensor_copy(out=dm.t[:, i, :], in_=st.t[:, 0:512]), [st.d], [dm.d])
    cx = C.sb("cmx_sb", [128, 4], F32)
    P.dma(cx.t[:], cmx_d.ap(), writes=[cx.d])
    dmb = C.sb("dmaskb_sb", [128, 4, 512], BF16)
    P.op("dve", lambda e: e.tensor_scalar(out=dmb.t[:], in0=dm.t[:], scalar1=-1.0, scalar2=NEGB, op0=ALU.add, op1=ALU.mult), [dm.d], [dmb.d])
    bx = C.sb("bx_sb", [128, 4], F32)
    P.op("dve", lambda e: e.tensor_scalar(out=bx.t[:], in0=cx.t[:], scalar1=-1.0, scalar2=NEGB, op0=ALU.add, op1=ALU.mult), [cx.d], [bx.d])
    return dmb, bx


def key_units(C, gq, j):
    half = C.NT // 2
    res = []
    for rel in range(2):
        g = gq if rel == 0 else 1 - gq
        for jj in range(j + 1):
            for i in range(4):
                m = None
                if jj == j:
                    m = ("d", i) if rel == 0 else ("x", gq * 2 + j % 2)
                res.append(((g * half + jj) * 4 + i, m))
    return res


def apply_mask(C, pt, m, dm, cx, view=None, nrep=1):
    P = C.P
    if m is None:
        return
    ap = pt.t[:] if view is None else view
    if m[0] == "d":
        if nrep == 1:
            P.op("dve", lambda e: e.tensor_tensor(out=ap, in0=ap, in1=dm.t[:, m[1], :], op=ALU.mult), [pt.d, dm.d], [pt.d])
        else:
            ap3 = ap.rearrange("p (r q) -> p r q", r=nrep)
            P.op("dve", lambda e: e.tensor_tensor(out=ap3, in0=ap3, in1=bcast_ap(dm.t[:, m[1], :], 1, nrep), op=ALU.mult), [pt.d, dm.d], [pt.d])
    else:
        P.op("dve", lambda e: e.tensor_scalar(out=ap, in0=ap, scalar1=cx.t[:, m[1]:m[1] + 1], scalar2=None, op0=ALU.mult), [pt.d, cx.d], [pt.d])


def diff_attn_phase(C, KA, VA, QA, OA, a_lambda, a_subln, dmask_d, cmx_d, pre=None):
    P, sb, ps = C.P, C.sb, C.ps
    phase_begin(C)
    if pre is not None:
        pre()
    S, NO, NKT = C.S, C.NT, C.NKT
    half = C.NT // 2
    dm, cx = load_cmask(C, dmask_d, cmx_d)
    lam_t = sb("lam_t", [128, 4, HD], F32)
    P.dma(lam_t.t[:], bass.AP(a_lambda, 0, [[0, 128], [HD, 4], [1, HD]]), writes=[lam_t.d])
    lp = sb("lp", [128, 2, HD], F32)
    lsum = sb("lsum", [128, 2], F32)
    neglam = sb("neglam", [128, 1], F32)
    gsub = sb("gsub", [128, 1], F32)
    P.op("dve", lambda e: e.tensor_tensor(out=lp.t[:, 0, :], in0=lam_t.t[:, 0, :], in1=lam_t.t[:, 1, :], op=ALU.mult), [lam_t.d], [lp.d])
    P.op("dve", lambda e: e.tensor_tensor(out=lp.t[:, 1, :], in0=lam_t.t[:, 2, :], in1=lam_t.t[:, 3, :], op=ALU.mult), [lam_t.d], [lp.d])
    P.op("dve", lambda e: e.tensor_reduce(out=lsum.t[:], in_=lp.t[:], axis=AX.X, op=ALU.add), [lp.d], [lsum.d])
    P.op("act", lambda e: e.activation(out=lsum.t[:], in_=lsum.t[:], func=AF.Exp), [lsum.d], [lsum.d])
    P.op("dve", lambda e: e.tensor_tensor(out=neglam.t[:], in0=lsum.t[:, 1:2], in1=lsum.t[:, 0:1], op=ALU.subtract), [lsum.d], [neglam.d])
    P.op("dve", lambda e: e.tensor_scalar(out=neglam.t[:], in0=neglam.t[:], scalar1=-LAM_INIT, scalar2=None, op0=ALU.add), [neglam.d], [neglam.d])
    P.dma(gsub.t[:], a_subln.ap()[0].rearrange("(p o) -> p o", o=1), writes=[gsub.d])
    P.op("dve", lambda e: e.tensor_scalar(out=gsub.t[:], in0=gsub.t[:], scalar1=1.0 - LAM_INIT, scalar2=None, op0=ALU.mult), [gsub.d], [gsub.d])

    Kh = [sb("Kh%d" % i, [128, S], BF16) for i in range(2)]
    Vh = [sb("Vh%d" % i, [128, NKT, 128], BF16) for i in range(2)]
    Qt = [sb("Qt%d" % i, [128, 512], BF16) for i in range(2)]
    Pt = [sb("Pt%d" % i, [128, 1024], BF16) for i in range(6)]
    ep = {n: sb("ep_" + n, [128, 512], F32) for n in ["R1", "R2", "O1", "O2", "o1", "o2", "o", "sq", "ln", "rstd"]}
    ot = [sb("ot%d" % i, [128, 512], BF16) for i in range(2)]
    acc = [sb("acc%d" % g, [128, 1024], F32) for g in range(2)]
    accS = sb("accS", [128, 1024], F32)
    pend = [None, None]
    pS = [ps("pS%d" % i, [128, 1024], F32) for i in range(2)]
    pO = [ps("pO%d" % i, [128, 512], F32) for i in range(2)]
    pL = [ps("pL%d" % i, [128, 512], F32) for i in range(2)]
    pM = pL[0]

    def load_kv(h):
        k, v = Kh[h % 2], Vh[h % 2]
        for t0 in range(0, S, 2048):
            t1_ = min(S, t0 + 2048)
            P.dma(k.t[:, t0:t1_], KA.ap()[h, :, t0:t1_], writes=[k.d])
        for kt0 in range(0, NKT, 16):
            P.dma(v.t[:, kt0:kt0 + 16, :], VA.ap()[kt0 * 128:(kt0 + 16) * 128, h * 128:(h + 1) * 128].rearrange("(t p) e -> p t e", p=128), writes=[v.d])

    def load_q(h, j):
        q = Qt[(h * NO + j) % 2]
        P.dma(q.t[:], QA.ap()[h, :, j * 512:(j + 1) * 512], writes=[q.d])

    load_kv(0)
    load_q(0, 0)
    gu = 0
    for h in range(6):
        if h + 1 < 6:
            load_kv(h + 1)
        k, v = Kh[h % 2], Vh[h % 2]
        for j in range(NO):
            nxt = h * NO + j + 1
            if nxt < 6 * NO:
                load_q(nxt // NO, nxt % NO)
            q = Qt[(h * NO + j) % 2]
            ku = key_units(C, j // half, j % half)
            n = len(ku)

            def qk(p):
                kt, m = ku[p]
                s2 = pS[(gu + p) % 2]
                dg = (m is not None and m[0] == "d")
                for c in range(2):
                    P.op("pe", lambda e: e.matmul(s2.t[:, c * 512:(c + 1) * 512], lhsT=k.t[c * 64:(c + 1) * 64, kt * 128:(kt + 1) * 128], rhs=q.t[c * 64:(c + 1) * 64, :], start=True, stop=not dg),
                         [k.d, q.d], [s2.d])
                    if dg:
                        P.op("pe", lambda e: e.matmul(s2.t[:, c * 512:(c + 1) * 512], lhsT=C.idb.t[:], rhs=dm.t[:, m[1], :], start=False, stop=True),
                             [C.idb.d, dm.d], [s2.d])

            qk(0)
            qk(1)
            for p in range(n):
                kt, m = ku[p]
                s2, pt = pS[(gu + p) % 2], Pt[(gu + p) % 6]
                if m is not None and m[0] == "x":
                    P.op("act", lambda e: e.activation(out=pt.t[:], in_=s2.t[:], func=AF.Exp, bias=cx.t[:, m[1]:m[1] + 1]), [s2.d, cx.d], [pt.d])
                else:
                    P.op("act", lambda e: e.activation(out=pt.t[:], in_=s2.t[:], func=AF.Exp), [s2.d], [pt.d])
                for c in range(2):
                    P.op("pe", lambda e: e.matmul(pO[c].t[:], lhsT=v.t[:, kt, :], rhs=pt.t[:, c * 512:(c + 1) * 512], start=(p == 0), stop=(p == n - 1)), [v.d, pt.d], [pO[c].d])
                g = 0 if p % 4 == 0 else 1
                ac = acc[g]
                aeng = ("pool", "dve")[g]
                if p < 2:
                    P.op(aeng, lambda e: e.tensor_copy(out=ac.t[:], in_=pt.t[:]), [pt.d], [ac.d])
                else:
                    P.op(aeng, lambda e: e.tensor_tensor(out=ac.t[:], in0=ac.t[:], in1=pt.t[:], op=ALU.add), [pt.d, ac.d], [ac.d])
                if p + 2 < n:
                    qk(p + 2)
                if p == 1 and pend[0] is not None:
                    pend[0]()
                    pend[0] = None
                if p == 5 and pend[1] is not None:
                    pend[1]()
                    pend[1] = None
            gu += n
            P.op("dve", lambda e: e.tensor_tensor(out=accS.t[:], in0=acc[0].t[:], in1=acc[1].t[:], op=ALU.add), [acc[0].d, acc[1].d], [accS.d])
            P.op("act", lambda e: e.activation(out=ep["O1"].t[:], in_=pO[0].t[:], func=AF.Copy), [pO[0].d], [ep["O1"].d])
            P.op("act", lambda e: e.activation(out=ep["O2"].t[:], in_=pO[1].t[:], func=AF.Copy), [pO[1].d], [ep["O2"].d])

            def part2a():
                for c in range(2):
                    P.op("pe", lambda e: e.matmul(pL[c].t[:], lhsT=C.ones1.t[:], rhs=accS.t[:, c * 512:(c + 1) * 512], start=True, stop=True), [C.ones1.d, accS.d], [pL[c].d])
                for c, rn in ((0, "R1"), (1, "R2")):
                    P.op("act", lambda e: e.activation(out=ep[rn].t[:], in_=pL[c].t[:], func=AF.Ln), [pL[c].d], [ep[rn].d])
                    P.op("act", lambda e: e.activation(out=ep[rn].t[:], in_=ep[rn].t[:], func=AF.Exp, scale=-1.0), [ep[rn].d], [ep[rn].d])
                P.op("pool", lambda e: e.tensor_tensor(out=ep["o1"].t[:], in0=ep["O1"].t[:], in1=ep["R1"].t[:], op=ALU.mult), [ep["O1"].d, ep["R1"].d], [ep["o1"].d])
                P.op("dve", lambda e: e.tensor_tensor(out=ep["o2"].t[:], in0=ep["O2"].t[:], in1=ep["R2"].t[:], op=ALU.mult), [ep["O2"].d, ep["R2"].d], [ep["o2"].d])
                P.op("dve", lambda e: e.scalar_tensor_tensor(out=ep["o"].t[:], in0=ep["o2"].t[:], scalar=neglam.t[:, 0:1], in1=ep["o1"].t[:], op0=ALU.mult, op1=ALU.add),
                     [ep["o2"].d, ep["o1"].d, neglam.d], [ep["o"].d])
                P.op("pool", lambda e: e.tensor_tensor(out=ep["sq"].t[:], in0=ep["o"].t[:], in1=ep["o"].t[:], op=ALU.mult), [ep["o"].d], [ep["sq"].d])

            def part2b(h=h, j=j):
                P.op("pe", lambda e: e.matmul(pM.t[:], lhsT=C.onesf.t[:], rhs=ep["sq"].t[:], start=True, stop=True), [C.onesf.d, ep["sq"].d], [pM.d])
                P.op("act", lambda e: e.activation(out=ep["ln"].t[:], in_=pM.t[:], func=AF.Ln, bias=EPS), [pM.d], [ep["ln"].d])
                P.op("act", lambda e: e.activation(out=ep["rstd"].t[:], in_=ep["ln"].t[:], func=AF.Exp, scale=-0.5), [ep["ln"].d], [ep["rstd"].d])
                o16 = ot[(h * NO + j) % 2]
                P.op("dve", lambda e: e.scalar_tensor_tensor(out=o16.t[:], in0=ep["o"].t[:], scalar=gsub.t[:, 0:1], in1=ep["rstd"].t[:], op0=ALU.mult, op1=ALU.mult),
                     [ep["o"].d, ep["rstd"].d, gsub.d], [o16.d])
                P.dma(OA.ap()[h, :, j * 512:(j + 1) * 512], o16.t[:], reads=[o16.d], writes=[Dep()])

            pend[0], pend[1] = part2a, part2b
    for fn in pend:
        if fn is not None:
            fn()
    phase_end(C)


def moba_attn_phase(C, KBf, VBf, QB, OB, onehot_d, dmask_d, cmx_d, pre=None):
    P, sb, ps = C.P, C.sb, C.ps
    phase_begin(C)
    if pre is not None:
        pre()
    S, NO, NKT = C.S, C.NO, C.NKT
    dm, cx = load_cmask(C, dmask_d, cmx_d)
    Kh = [sb("Kh%d" % i, [128, S], BF16) for i in range(2)]
    Vh = [sb("Vh%d" % i, [128, NKT, 128], BF16) for i in range(2)]
    Osb = [sb("Osb%d" % i, [128, 512], F32) for i in range(2)]
    for b in range(2):
        P.op("pool", lambda e: e.memset(Vh[b].t[:, :, 64:128], 1.0), [], [Vh[b].d])
    Qt = [sb("Qt%d" % i, [128, 512], BF16) for i in range(2)]
    Pt = [sb("Pt%d" % i, [128, 1024], BF16) for i in range(4)]
    R = sb("R", [64, 512], F32)
    ot = [sb("ot%d" % i, [64, 512], BF16) for i in range(2)]
    pS = [ps("pS%d" % i, [128, 1024], F32) for i in range(3)]
    pO = ps("pO", [128, 512], F32)
    pL = ps("pL", [128, 512], F32)
    for b in range(2):
        for n0 in range(0, S, 1024):
            n1 = min(S, n0 + 1024)
            st = next_stg(C)
            P.dma(st.t[64:96, 0:n1 - n0], onehot_d.ap()[:, n0:n1], writes=[st.d])
            P.op("dve", lambda e, st=st, b=b, n0=n0, n1=n1: e.tensor_copy(out=Kh[b].t[64:96, n0:n1], in_=st.t[64:96, 0:n1 - n0]), [st.d], [Kh[b].d])

    def load_kv(h):
        k, v = Kh[h % 2], Vh[h % 2]
        for t0 in range(0, S, 2048):
            t1_ = min(S, t0 + 2048)
            P.dma(k.t[0:64, t0:t1_], KBf.ap()[h // 2, (h % 2) * 64:(h % 2) * 64 + 64, t0:t1_], writes=[k.d])
        for kt0 in range(0, NKT, 16):
            P.dma(v.t[:, kt0:kt0 + 16, 0:64], VBf.ap()[kt0 * 128:(kt0 + 16) * 128, h * HD:(h + 1) * HD].rearrange("(t p) e -> p t e", p=128), writes=[v.d])

    def load_q(h, j):
        q = Qt[(h * NO + j) % 2]
        P.dma(q.t[0:96, :], QB.ap()[h, :, j * 512:(j + 1) * 512], writes=[q.d])

    load_kv(0)
    load_q(0, 0)
    gu = 0
    for h in range(12):
        if h + 1 < 12:
            load_kv(h + 1)
        k, v = Kh[h % 2], Vh[h % 2]
        for j in range(NO):
            nxt = h * NO + j + 1
            if nxt < 12 * NO:
                load_q(nxt // NO, nxt % NO)
            q = Qt[(h * NO + j) % 2]
            units = key_units(C, 0, j)
            npair = len(units) // 2

            def qk(p):
                s2 = pS[(gu + p) % 3]
                for hf in range(2):
                    kt, mk = units[2 * p + hf]
                    dg = (mk is not None and mk[0] == "d")
                    P.op("pe", lambda e: e.matmul(s2.t[:, hf * 512:(hf + 1) * 512], lhsT=k.t[0:96, kt * 128:(kt + 1) * 128], rhs=q.t[0:96, :], start=True, stop=not dg),
                         [k.d, q.d], [s2.d])
                    if dg:
                        P.op("pe", lambda e: e.matmul(s2.t[:, hf * 512:(hf + 1) * 512], lhsT=C.idb.t[:], rhs=dm.t[:, mk[1], :], start=False, stop=True),
                             [C.idb.d, dm.d], [s2.d])

            qk(0)
            qk(1)
            qk(2)
            for p in range(npair):
                s2, pt = pS[(gu + p) % 3], Pt[(gu + p) % 4]
                mk0, mk1 = units[2 * p][1], units[2 * p + 1][1]
                if mk0 is not None and mk0[0] == "x":
                    assert mk1 == mk0
                    P.op("act", lambda e: e.activation(out=pt.t[:], in_=s2.t[:], func=AF.Exp, bias=cx.t[:, mk0[1]:mk0[1] + 1]), [s2.d, cx.d], [pt.d])
                else:
                    P.op("act", lambda e: e.activation(out=pt.t[:], in_=s2.t[:], func=AF.Exp), [s2.d], [pt.d])
                for hf in range(2):
                    kt = units[2 * p + hf][0]
                    P.op("pe", lambda e: e.matmul(pO.t[:], lhsT=v.t[:, kt, :], rhs=pt.t[:, hf * 512:(hf + 1) * 512], start=(p == 0 and hf == 0), stop=(p == npair - 1 and hf == 1)),
                         [v.d, pt.d], [pO.d])
                if p + 3 < npair:
                    qk(p + 3)
            nkt = npair
            gu += nkt
            o16 = ot[(h * NO + j) % 2]
            osb = Osb[(h * NO + j) % 2]
            P.op("act", lambda e: e.activation(out=osb.t[:], in_=pO.t[:], func=AF.Copy), [pO.d], [osb.d])
            P.op("pe", lambda e: e.matmul(pL.t[0:64, :], lhsT=C.idf.t[:, 64:128], rhs=osb.t[:], start=True, stop=True), [C.idf.d, osb.d], [pL.d])
            P.op("dve", lambda e: e.reciprocal(out=R.t[:], in_=pL.t[0:64, :]), [pL.d], [R.d])
            P.op("dve", lambda e, o16=o16: e.tensor_tensor(out=o16.t[:], in0=osb.t[0:64, :], in1=R.t[:], op=ALU.mult), [osb.d, R.d], [o16.d])
            P.dma(OB.ap()[h, :, j * 512:(j + 1) * 512], o16.t[:], reads=[o16.d], writes=[Dep()])
    phase_end(C)


def attn_out_phase(C, contribs, w_out, x_src, x_dst, ntiles):
    P, sb, ps = C.P, C.sb, C.ps
    phase_begin(C)
    NO = ntiles
    nct = len(contribs)
    Wo = sb("Wo", [128, nct, D], BF16)
    for ci, (_, Kp, row0) in enumerate(contribs):
        st = next_stg(C)
        P.dma(st.t[0:Kp, :], w_out[row0:row0 + Kp, :], writes=[st.d])
        cast_op(C, Wo.t[0:Kp, ci, :], st.t[0:Kp, :], [st.d], [Wo.d])
    ot = [sb("ot%d" % i, [128, nct, 512], BF16) for i in range(2)]
    xres = [sb("xres%d" % i, [128, 4, D], F32) for i in range(2)]
    pA = [ps("pA%d" % i, [128, 512], F32) for i in range(2)]

    def loads(j):
        o_, xr = ot[j % 2], xres[j % 2]
        for ci, (ap_, Kp, _) in enumerate(contribs):
            P.dma(o_.t[0:Kp, ci, :], ap_[:, j * 512:(j + 1) * 512], writes=[o_.d])
        P.dma(xr.t[:], x_src[j * 512:(j + 1) * 512, :].rearrange("(s p) d -> p s d", p=128), writes=[xr.d])

    loads(0)
    for j in range(NO):
        if j + 1 < NO:
            loads(j + 1)
        o_, xr = ot[j % 2], xres[j % 2]
        for s in range(4):
            for nh in range(2):
                pa = pA[(s * 2 + nh) % 2]
                for ci, (_, Kp, _) in enumerate(contribs):
                    P.op("pe", lambda e, ci=ci, Kp=Kp, s=s, nh=nh, pa=pa: e.matmul(pa.t[:], lhsT=o_.t[0:Kp, ci, s * 128:(s + 1) * 128], rhs=Wo.t[0:Kp, ci, nh * 512:(nh + 1) * 512],
                                                                                start=(ci == 0), stop=(ci == nct - 1)),
                         [o_.d, Wo.d], [pa.d])
                P.op("dve", lambda e, s=s, nh=nh, pa=pa: e.tensor_tensor(out=xr.t[:, s, nh * 512:(nh + 1) * 512], in0=pa.t[:], in1=xr.t[:, s, nh * 512:(nh + 1) * 512], op=ALU.add),
                     [pa.d, xr.d], [xr.d])
        P.dma(x_dst[j * 512:(j + 1) * 512, :].rearrange("(s p) d -> p s d", p=128), xr.t[:], reads=[xr.d], writes=[Dep()])
    phase_end(C)


def mlp_phase(C, w_up, w_down, gain, x_src, x_dst, ntiles, final_g=None, Wup=None):
    P, sb, ps = C.P, C.sb, C.ps
    phase_begin(C, nstg=2)
    Wdn = sb("Wdn", [128, 32, D], BF16)
    if Wup is None:
        Wup = sb("Wup", [128, 8, DFF], BF16)
        load_w(C, Wup, w_up, 8, DFF, gain)
    load_w(C, Wdn, w_down, 32, D, None)
    xres = [sb("xres%d" % i, [128, 2, D], F32) for i in range(2)]
    hT = [sb("hT%d" % i, [128, 8, 256], BF16) for i in range(2)]
    uT = sb("uT", [128, 32, 256], BF16)
    r = [sb("r%d" % i, [128, 256], F32) for i in range(3)]
    pT = [ps("pT0", [128, 1024], BF16)]
    pU = [ps("pU%d" % i, [128, 512], F32) for i in range(3)]
    pD = [ps("pD%d" % i, [128, 512], F32) for i in range(2)]
    if final_g is not None:
        gfin = sb("gfin", [128, D], F32)
        P.dma(gfin.t[:], bass.AP(final_g, 0, [[0, 128], [1, D]]), writes=[gfin.d])
        yo = sb("yo", [128, 2, D], F32)

    def loads(T):
        xr = xres[T % 2]
        P.dma(xr.t[:], x_src[T * 256:(T + 1) * 256, :].rearrange("(s p) d -> p s d", p=128), writes=[xr.d])

    loads(0)
    norm_T(C, xres[0], 2, hT[0], pT)
    for T in range(ntiles):
        if T + 1 < ntiles:
            loads(T + 1)
        xr, h = xres[T % 2], hT[T % 2]
        for f in range(32):
            pu, rr = pU[f % 3], r[f % 3]
            for kc in range(8):
                P.op("pe", lambda e, kc=kc, f=f, pu=pu: e.matmul(pu.t[:, 0:256], lhsT=Wup.t[:, kc, f * 128:(f + 1) * 128], rhs=h.t[:, kc, :], start=(kc == 0), stop=(kc == 7)),
                     [Wup.d, h.d], [pu.d])
            P.op("act", lambda e, pu=pu, rr=rr: e.activation(out=rr.t[:], in_=pu.t[:, 0:256], func=AF.Relu), [pu.d], [rr.d])
            P.op("pool", lambda e, f=f, rr=rr: e.tensor_tensor(out=uT.t[:, f, :], in0=rr.t[:], in1=rr.t[:], op=ALU.mult), [rr.d], [uT.d])
        if T + 1 < ntiles:
            norm_T(C, xres[(T + 1) % 2], 2, hT[(T + 1) % 2], pT)
        for s in range(2):
            for nh in range(2):
                pd = pD[(s * 2 + nh) % 2]
                for f in range(32):
                    P.op("pe", lambda e, f=f, s=s, nh=nh, pd=pd: e.matmul(pd.t[:], lhsT=uT.t[:, f, s * 128:(s + 1) * 128], rhs=Wdn.t[:, f, nh * 512:(nh + 1) * 512], start=(f == 0), stop=(f == 31)),
                         [uT.d, Wdn.d], [pd.d])
                P.op("dve", lambda e, s=s, nh=nh, pd=pd: e.tensor_tensor(out=xr.t[:, s, nh * 512:(nh + 1) * 512], in0=pd.t[:], in1=xr.t[:, s, nh * 512:(nh + 1) * 512], op=ALU.add),
                     [pd.d, xr.d], [xr.d])
        dst = x_dst[T * 256:(T + 1) * 256, :].rearrange("(s p) d -> p s d", p=128)
        if final_g is None:
            P.dma(dst, xr.t[:], reads=[xr.d], writes=[Dep()])
        else:
            ss, rs = C.ss, C.rs
            for s in range(2):
                P.op("act", lambda e, s=s: e.activation(out=C.junk.t[:], in_=xr.t[:, s, :], func=AF.Square, accum_out=ss.t[:, s:s + 1]), [xr.d], [C.junk.d, ss.d])
            P.op("act", lambda e: e.activation(out=ss.t[:, 0:2], in_=ss.t[:, 0:2], func=AF.Sqrt, scale=1.0 / D, bias=EPS), [ss.d], [ss.d])
            P.op("dve", lambda e: e.reciprocal(out=rs.t[:, 0:2], in_=ss.t[:, 0:2]), [ss.d], [rs.d])
            for s in range(2):
                P.op("dve", lambda e, s=s: e.scalar_tensor_tensor(out=yo.t[:, s, :], in0=xr.t[:, s, :], scalar=rs.t[:, s:s + 1], in1=gfin.t[:], op0=ALU.mult, op1=ALU.mult),
                     [xr.d, rs.d, gfin.d], [yo.d])
            P.dma(dst, yo.t[:], reads=[yo.d], writes=[Dep()])
    phase_end(C)


def build(S, dbg=False, upto=99):
    nc = bass.Bass("TRN2", target_bir_lowering=False)
    C = Ctx()
    C.nc = nc
    C.P = P = Prog(nc)
    NT = S // 512
    NO = NT // 2
    SO = NO * 512
    NKT = S // 128
    C.S, C.NT, C.NO, C.SO, C.NKT, C.NB = S, NT, NO, SO, NKT, S // 256

    def din(name, shape, dt=F32):
        return nc.dram_tensor(name, list(shape), dt, kind="ExternalInput")

    def dout(name, shape, dt=F32):
        return nc.dram_tensor(name, list(shape), dt, kind="ExternalOutput")

    def dscr(name, shape, dt=BF16):
        if dbg:
            return nc.dram_tensor(name, list(shape), dt, kind="ExternalOutput")
        return nc.dram_tensor(name, list(shape), dt)

    uid = [0]

    def sb(name, shape, dt):
        uid[0] += 1
        return Tile(nc.alloc_sbuf_tensor("%s_%d" % (name, uid[0]), list(shape), dt))

    def ps(name, shape, dt=F32):
        uid[0] += 1
        return Tile(nc.alloc_psum_tensor("%s_%d" % (name, uid[0]), list(shape), dt))

    C.sb, C.ps = sb, ps
    ident_d = din("ident", [128, 128])
    dmask_d = din("dmask", [4, 128, 512])
    cmx_d = din("cmx", [128, 4])
    mem_d = din("mem", [MEM, D])
    ropeK_d = din("ropeK", [2, 128, S])
    ropeQ_d = din("ropeQ", [2, 128, S])
    onehot_d = din("onehot", [32, S])
    gb_d = din("gate_gb", [SO, 32])
    t3_d = din("gate_t3", [SO, 32])
    x_perm = din("x_perm", [S, D])
    W = {}
    for name, shape in [("a_norm_attn", [1, D]), ("a_w_in", [1, D, 3 * SW + MW]), ("a_lambda", [1, 4, HD]), ("a_subln", [1, 128]),
                        ("a_mem_norm", [1, D]), ("a_w_mem_kv", [1, D, 2 * MW]), ("a_w_out", [1, D, D]), ("a_norm_mlp", [1, D]),
                        ("a_w_up", [1, D, DFF]), ("a_w_down", [1, DFF, D]), ("kv_norm", [D]), ("w_kv", [D, 2 * SW]),
                        ("b_norm_attn", [1, D]), ("b_w_in", [1, D, SW + MW]), ("b_mem_norm", [1, D]), ("b_w_mem_kv", [1, D, 2 * MW]),
                        ("b_w_out", [1, D, D]), ("b_norm_mlp", [1, D]), ("b_w_up", [1, D, DFF]), ("b_w_down", [1, DFF, D]), ("final_norm", [D])]:
        W[name] = din(name, shape)
    OUT = dout("out", [SO, D], F32)

    C.idf = sb("idf", [128, 128], F32)
    C.idb = sb("idb", [128, 128], BF16)
    C.onesb = sb("onesb", [128, 128], BF16)
    C.onesf = sb("onesf", [128, 128], F32)
    C.ones1 = sb("ones1", [128, 128], F32)
    C.ss = sb("ss", [128, 4], F32)
    C.rs = sb("rs", [128, 4], F32)
    C.junk = sb("junk", [128, 1024], BF16)
    C.xb = [sb("xb%d" % i, [128, 1024], BF16) for i in range(2)]
    P.dma(C.idf.t[:], ident_d.ap(), writes=[C.idf.d])
    P.op("dve", lambda e: e.tensor_copy(out=C.idb.t[:], in_=C.idf.t[:]), [C.idf.d], [C.idb.d])
    P.op("pool", lambda e: e.memset(C.onesb.t[:], 1.0), [], [C.onesb.d])
    P.op("pool", lambda e: e.memset(C.onesf.t[:], 1.0 / 128.0), [], [C.onesf.d])
    P.op("pool", lambda e: e.memset(C.ones1.t[:], 1.0), [], [C.ones1.d])

    def load_gain(name, ap1d):
        g = sb("g_" + name, [128, 8], F32)
        P.dma(g.t[:], ap1d.rearrange("(c p) -> p c", p=128), writes=[g.d], allow_slow_non_contiguous=True)
        return g

    ga_attn = load_gain("a_attn", W["a_norm_attn"].ap()[0])
    ga_mem = load_gain("a_mem", W["a_mem_norm"].ap()[0])
    ga_mlp = load_gain("a_mlp", W["a_norm_mlp"].ap()[0])
    g_kv = load_gain("kv", W["kv_norm"].ap())
    gb_attn = load_gain("b_attn", W["b_norm_attn"].ap()[0])
    gb_mem = load_gain("b_mem", W["b_mem_norm"].ap()[0])
    gb_mlp = load_gain("b_mlp", W["b_norm_mlp"].ap()[0])
    C.sb_mark = nc.sbuf_base
    C.ps_mark = nc.psum_base

    KA = dscr("KA", [6, 128, S])
    VA = dscr("VA", [S, SW])
    QA = dscr("QA", [6, 128, S])
    QMa = dscr("QMa", [2, 128, S])
    OA = dscr("OA", [6, 128, S])
    OMa = dscr("OMa", [4, 64, S])
    X1a = dscr("X1a", [S, D], F32)
    X1 = dscr("X1", [S, D], F32)
    KB = dscr("KB", [6, 128, S])
    VB = dscr("VB", [S, SW])
    KM = dscr("KM", [6, 128, 32], F32)
    QB = dscr("QB", [12, 96, SO])
    QMb = dscr("QMb", [2, 128, SO])
    OB = dscr("OB", [12, 64, SO])
    OMb = dscr("OMb", [4, 64, SO])
    X2a = dscr("X2a", [SO, D], F32)

    w_in = W["a_w_in"].ap()[0]
    proj_phase(C, x_perm.ap(), NT, ga_attn,
               fm=[dict(w=w_in[:, SW + 128 * h: SW + 128 * (h + 1)], rope=0, out=KA.ap()[h]) for h in range(6)]
               + [dict(w=w_in[:, 128 * h:128 * (h + 1)], rope=1, out=QA.ap()[h]) for h in range(6)]
               + [dict(w=w_in[:, 3 * SW + 128 * c:3 * SW + 128 * (c + 1)], rope=None, scale=0.125, out=QMa.ap()[c]) for c in range(2)],
               rope_d=[ropeK_d, ropeQ_d], tm=dict(w=w_in[:, 2 * SW:3 * SW], out=VA.ap()))
    if upto >= 2:
        mem_phase(C, mem_d, W["a_w_mem_kv"].ap()[0], ga_mem, QMa, OMa, NT)
    base_mark = C.sb_mark
    WupA = None
    if upto >= 3:
        WupA = sb("WupA", [128, 8, DFF], BF16)
        C.sb_mark = nc.sbuf_base
        diff_attn_phase(C, KA, VA, QA, OA, W["a_lambda"], W["a_subln"], dmask_d, cmx_d,
                        pre=lambda: load_w(C, WupA, W["a_w_up"].ap()[0], 8, DFF, ga_mlp))
    if upto >= 4:
        attn_out_phase(C, [(OA.ap()[h], 128, 128 * h) for h in range(6)] + [(OMa.ap()[m], 64, SW + 64 * m) for m in range(4)],
                       W["a_w_out"].ap()[0], x_perm.ap(), X1a.ap(), NT)
    if upto >= 5:
        mlp_phase(C, W["a_w_up"].ap()[0], W["a_w_down"].ap()[0], ga_mlp, X1a.ap(), X1.ap(), S // 256, final_g=None, Wup=WupA)
    C.sb_mark = base_mark
    nc.sbuf_base = base_mark
    if upto >= 6:
        w_kv = W["w_kv"].ap()
        proj_phase(C, X1.ap(), NT, g_kv,
                   fm=[dict(w=w_kv[:, 128 * c:128 * (c + 1)], rope=0, out=KB.ap()[c], kmean=KM.ap()[c]) for c in range(6)],
                   rope_d=[ropeK_d], tm=dict(w=w_kv[:, SW:2 * SW], out=VB.ap()))
    if upto >= 7:
        w_in = W["b_w_in"].ap()[0]
        proj_phase(C, X1.ap(), NO, gb_attn,
                   fm=[dict(w=w_in[:, 128 * c:128 * (c + 1)], rope=0, gate=c) for c in range(6)]
                   + [dict(w=w_in[:, SW + 128 * c:SW + 128 * (c + 1)], rope=None, scale=0.125, out=QMb.ap()[c]) for c in range(2)],
                   rope_d=[ropeQ_d], tm=None, gate=dict(km=KM, gb=gb_d, t3=t3_d, QB=QB))
    if upto >= 8:
        mem_phase(C, mem_d, W["b_w_mem_kv"].ap()[0], gb_mem, QMb, OMb, NO)
    WupB = None
    if upto >= 9:
        WupB = sb("WupB", [128, 8, DFF], BF16)
        C.sb_mark = nc.sbuf_base
        moba_attn_phase(C, KB, VB, QB, OB, onehot_d, dmask_d, cmx_d,
                        pre=lambda: load_w(C, WupB, W["b_w_up"].ap()[0], 8, DFF, gb_mlp))
    if upto >= 10:
        attn_out_phase(C, [(OB.ap()[h], 64, 64 * h) for h in range(12)] + [(OMb.ap()[m], 64, SW + 64 * m) for m in range(4)],
                       W["b_w_out"].ap()[0], X1.ap(), X2a.ap(), NO)
    if upto >= 11:
        mlp_phase(C, W["b_w_up"].ap()[0], W["b_w_down"].ap()[0], gb_mlp, X2a.ap(), OUT.ap(), SO // 256, final_g=W["final_norm"], Wup=WupB)

    P.barrier()
    P.emit()
    return nc


def rope_table(pos, scale=1.0):
    half = HD // 2
    inv = (1.0 / (10000.0 ** (np.arange(half, dtype=np.float32) / np.float32(half)))).astype(np.float32)
    ang = pos.astype(np.float32)[:, None] * inv[None, :]
    cos = np.cos(ang).astype(np.float32)
    sin = np.sin(ang).astype(np.float32)
    p = np.arange(128)
    sign = np.where((p % 64) < 32, -1.0, 1.0).astype(np.float32)
    tab = np.empty((2, 128, len(pos)), np.float32)
    tab[0] = cos.T[p % 32, :] * np.float32(scale)
    tab[1] = sin.T[p % 32, :] * sign[:, None] * np.float32(scale)
    return tab


def core_tables(S, r):
    NT = S // 512
    half = NT // 2
    groups = [own_tiles(NT, r), own_tiles(NT, 1 - r)]
    perm_tiles = groups[0] + groups[1]
    pos = np.concatenate([np.arange(t * 512, (t + 1) * 512) for t in perm_tiles])
    pos_own = pos[:half * 512]
    cmx = np.zeros((128, 4), np.float32)
    for gq in range(2):
        for par in range(2):
            vals = set()
            for j in range(par, half, 2):
                vals.add(1.0 if groups[1 - gq][j] < groups[gq][j] else 0.0)
            assert len(vals) == 1
            cmx[:, gq * 2 + par] = vals.pop()
    kk = np.arange(128)[:, None]
    q = np.arange(512)[None, :]
    dmask = np.stack([(128 * i + kk <= q).astype(np.float32) for i in range(4)])
    blk = (pos[::256] // 256)
    nblk = len(blk)
    cur = pos_own // 256
    gb = np.full((len(pos_own), 32), -1e30, np.float32)
    t3 = np.full((len(pos_own), 32), -1.0, np.float32)
    gb[:, :nblk] = np.where(blk[None, :] < cur[:, None], 0.0, -1e30)
    t3[:, :nblk] = np.where(blk[None, :] == cur[:, None], 0.0, -1.0)
    onehot = np.zeros((32, S), np.float32)
    onehot[np.arange(S) // 256, np.arange(S)] = 1.0
    return dict(pos=pos, pos_own=pos_own, cmx=cmx, dmask=dmask, gate_gb=gb, gate_t3=t3, onehot=onehot,
                ropeK=rope_table(pos, 1.0), ropeQ=rope_table(pos, 0.125))


_cache = {}
W_NAMES = ["a_norm_attn", "a_w_in", "a_lambda", "a_subln", "a_mem_norm", "a_w_mem_kv", "a_w_out", "a_norm_mlp", "a_w_up", "a_w_down", "kv_norm", "w_kv",
           "b_norm_attn", "b_w_in", "b_mem_norm", "b_w_mem_kv", "b_w_out", "b_norm_mlp", "b_w_up", "b_w_down", "final_norm"]


def get_prog(S):
    if S not in _cache:
        _cache[S] = build(S)
    return _cache[S]


def make_in_maps(inp, S, B):
    x = np.ascontiguousarray(inp["x"], dtype=np.float32)
    mem = np.ascontiguousarray(inp["mem"], dtype=np.float32)
    ident = np.eye(128, dtype=np.float32)
    tabs = [core_tables(S, r) for r in range(2)]
    ws = {k: np.ascontiguousarray(inp[k], dtype=np.float32) for k in W_NAMES}
    maps = []
    for c in range(2 * B):
        b, r = c // 2, c % 2
        tb = tabs[r]
        m = dict(ws)
        m.update(ident=ident, dmask=tb["dmask"], cmx=tb["cmx"], mem=mem[b], ropeK=tb["ropeK"], ropeQ=tb["ropeQ"], onehot=tb["onehot"],
                 gate_gb=tb["gate_gb"], gate_t3=tb["gate_t3"], x_perm=np.ascontiguousarray(x[b][tb["pos"]]))
        maps.append(m)
    return maps, tabs


def run_model(inp, S, B, nc=None):
    maps, tabs = make_in_maps(inp, S, B)
    if nc is None:
        nc = get_prog(S)
    res = run_bass_kernel_spmd(nc, maps, core_ids=list(range(2 * B))).results
    out = np.empty((B, S, D), np.float32)
    for c in range(2 * B):
        out[c // 2][tabs[c % 2]["pos_own"]] = res[c]["out"]
    return out, res


def kernel(**inputs):
    x = inputs["x"]
    B, S, _ = x.shape
    out, _ = run_model(inputs, S, B)
    return out
```
